# Optimizing a Trainium2 kernel written in Bass

```python
import jax
import jax.numpy as jnp
from jax import lax
import numpy as np

D_MODEL = 2048
BATCH = 2
SEQ = 4096
DEPTH = 1

CHUNK = 64
MLSTM_HEADS = 8
MLSTM_WIDTH = D_MODEL
MLSTM_HEAD_DIM = MLSTM_WIDTH // MLSTM_HEADS
QK_CONV_WIDTH = 4
POOL_GROUPS = 4
POOL_WIDTH = D_MODEL // 2
POOL_GROUP_DIM = POOL_WIDTH // POOL_GROUPS
POOL_WINDOWS = (2, 4, 8, 16)
RMS_EPS = 1e-6
IN_SPLITS = (MLSTM_WIDTH, MLSTM_WIDTH, MLSTM_WIDTH, MLSTM_WIDTH, MLSTM_WIDTH,
             MLSTM_HEADS, MLSTM_HEADS, POOL_WIDTH, POOL_WIDTH, D_MODEL, D_MODEL)
IN_WIDTH = 5 * MLSTM_WIDTH + 2 * MLSTM_HEADS + 2 * POOL_WIDTH + 2 * D_MODEL

kernel_name = 'hybrid_mlstm_pool_gated_block'


def _rmsnorm(x, w):
    xf = x.astype(jnp.float32)
    y = xf * lax.rsqrt(jnp.mean(xf * xf, axis=-1, keepdims=True) + RMS_EPS)
    return (y * w.astype(jnp.float32)).astype(x.dtype)


def _partition(t):
    parts, start = [], 0
    for width in IN_SPLITS:
        parts.append(t[..., start:start + width])
        start += width
    return parts


def _causal_depthwise_conv(x, w, b):
    c = x.shape[-1]
    y = lax.conv_general_dilated(
        x, w[:, None, :].astype(x.dtype), window_strides=(1,),
        padding=[(w.shape[0] - 1, 0)], dimension_numbers=('NWC', 'WIO', 'NWC'),
        feature_group_count=c)
    return y + b.astype(x.dtype)


def _to_chunks(t, nc):
    b = t.shape[0]
    t = t.reshape((b, nc, CHUNK) + t.shape[2:])
    t = jnp.moveaxis(t, 1, 0)
    return jnp.swapaxes(t, 2, 3)


def _mlstm_chunkwise(q, k, v, ig, lf):
    bsz, s, nh, d = q.shape
    nc = s // CHUNK
    causal = jnp.tril(jnp.ones((CHUNK, CHUNK), dtype=bool))
    xs = (_to_chunks(q, nc), _to_chunks(k, nc), _to_chunks(v, nc),
          _to_chunks(ig, nc), _to_chunks(lf, nc))

    def step(carry, inp):
        c_st, n_st, m_st = carry
        qc, kc, vc, ic, fc = inp
        b = jnp.cumsum(fc, axis=-1)
        log_d = b[..., :, None] - b[..., None, :] + ic[..., None, :]
        log_d = jnp.where(causal, log_d, -jnp.inf)
        log_inter = b + m_st[..., None]
        m_row = jnp.maximum(jnp.max(log_d, axis=-1), log_inter)
        p = jnp.exp(log_d - m_row[..., None]) * jnp.einsum('bhld,bhsd->bhls', qc, kc)
        w_inter = jnp.exp(log_inter - m_row)
        num = (jnp.einsum('bhls,bhsd->bhld', p, vc)
               + w_inter[..., None] * jnp.einsum('bhlk,bhkv->bhlv', qc, c_st))
        den = jnp.sum(p, axis=-1) + w_inter * jnp.einsum('bhlk,bhk->bhl', qc, n_st)
        h = num / jnp.maximum(jnp.abs(den), jnp.exp(-m_row))[..., None]
        g = b[..., -1]
        log_w = g[..., None] - b + ic
        m_new = jnp.maximum(g + m_st, jnp.max(log_w, axis=-1))
        decay = jnp.exp(g + m_st - m_new)
        w = jnp.exp(log_w - m_new[..., None])
        c_new = decay[..., None, None] * c_st + jnp.einsum('bhs,bhsk,bhsv->bhkv', w, kc, vc)
        n_new = decay[..., None] * n_st + jnp.einsum('bhs,bhsk->bhk', w, kc)
        return (c_new, n_new, m_new), h

    init = (jnp.zeros((bsz, nh, d, d), jnp.float32),
            jnp.zeros((bsz, nh, d), jnp.float32),
            jnp.zeros((bsz, nh), jnp.float32))
    _, hs = lax.scan(step, init, xs)
    hs = jnp.moveaxis(jnp.swapaxes(hs, 2, 3), 0, 1)
    return hs.reshape(bsz, s, nh, d)


def _multiscale_pool(u):
    bsz, s, _ = u.shape
    uf = u.astype(jnp.float32).reshape(bsz, s, POOL_GROUPS, POOL_GROUP_DIM)
    cs = jnp.concatenate(
        [jnp.zeros((bsz, 1, POOL_GROUPS, POOL_GROUP_DIM), jnp.float32),
         jnp.cumsum(uf, axis=1)], axis=1)
    t = jnp.arange(1, s + 1)[:, None]
    win = jnp.array(POOL_WINDOWS, dtype=jnp.int32)[None, :]
    lo = jnp.maximum(t - win, 0)
    grp = jnp.arange(POOL_GROUPS)[None, :]
    window_sum = cs[:, 1:] - cs[:, lo, grp]
    count = (t - lo).astype(jnp.float32)
    return window_sum / count[..., None] - uf


def setup_inputs(seed: int = 0) -> dict:
    key = jax.random.key(seed)
    ks = jax.random.split(key, 16)
    f32 = jnp.float32
    nrm = lambda k, shape, scale: (jax.random.normal(k, shape, f32) * scale)
    x = jax.random.normal(ks[0], (BATCH, SEQ, D_MODEL), f32)
    norm_pre_w = 1.0 + nrm(ks[1], (DEPTH, D_MODEL), 0.05)
    w_in = nrm(ks[2], (DEPTH, D_MODEL, IN_WIDTH), D_MODEL ** -0.5)
    mlstm_i_bias = nrm(ks[3], (DEPTH, MLSTM_HEADS), 0.1)
    mlstm_f_bias = (jnp.linspace(3.0, 6.0, MLSTM_HEADS, dtype=f32)[None, :]
                    + nrm(ks[4], (DEPTH, MLSTM_HEADS), 0.1))
    qk_conv_w = nrm(ks[5], (DEPTH, QK_CONV_WIDTH, 2 * MLSTM_WIDTH), QK_CONV_WIDTH ** -0.5)
    qk_conv_b = nrm(ks[6], (DEPTH, 2 * MLSTM_WIDTH), 0.01)
    mlstm_norm_w = 1.0 + nrm(ks[7], (DEPTH, MLSTM_WIDTH), 0.05)
    pool_w = nrm(ks[8], (DEPTH, POOL_GROUPS, POOL_GROUP_DIM, POOL_GROUP_DIM), POOL_GROUP_DIM ** -0.5)
    pool_scale = 1.0 + nrm(ks[9], (DEPTH, POOL_WIDTH), 0.1)
    w_proj_mlstm = nrm(ks[10], (DEPTH, MLSTM_WIDTH, D_MODEL), MLSTM_WIDTH ** -0.5)
    w_proj_pool = nrm(ks[11], (DEPTH, POOL_WIDTH, D_MODEL), POOL_WIDTH ** -0.5)
    w_out = nrm(ks[12], (DEPTH, D_MODEL, D_MODEL), D_MODEL ** -0.5)
    norm_post_w = 1.0 + nrm(ks[13], (DEPTH, D_MODEL), 0.05)
    return {'x': x, 'norm_pre_w': norm_pre_w, 'w_in': w_in,
            'mlstm_i_bias': mlstm_i_bias, 'mlstm_f_bias': mlstm_f_bias,
            'qk_conv_w': qk_conv_w, 'qk_conv_b': qk_conv_b,
            'mlstm_norm_w': mlstm_norm_w, 'pool_w': pool_w, 'pool_scale': pool_scale,
            'w_proj_mlstm': w_proj_mlstm, 'w_proj_pool': w_proj_pool,
            'w_out': w_out, 'norm_post_w': norm_post_w}


def reference(x, norm_pre_w, w_in, mlstm_i_bias, mlstm_f_bias, qk_conv_w, qk_conv_b,
              mlstm_norm_w, pool_w, pool_scale, w_proj_mlstm, w_proj_pool, w_out,
              norm_post_w):
    bsz, s, _ = x.shape
    f32 = jnp.float32
    for l in range(DEPTH):
        h = _rmsnorm(x, norm_pre_w[l])
        proj = jnp.einsum('bsd,de->bse', h, w_in[l])
        (q, k, v, o_g, z_a, i_pre, f_pre, u_b, z_b, g_a, g_b) = _partition(proj)

        qk = jax.nn.silu(_causal_depthwise_conv(jnp.concatenate([q, k], axis=-1),
                                                qk_conv_w[l], qk_conv_b[l]))
        q = qk[..., :MLSTM_WIDTH].astype(f32).reshape(bsz, s, MLSTM_HEADS, MLSTM_HEAD_DIM)
        k = qk[..., MLSTM_WIDTH:].astype(f32).reshape(bsz, s, MLSTM_HEADS, MLSTM_HEAD_DIM)
        v = v.astype(f32).reshape(bsz, s, MLSTM_HEADS, MLSTM_HEAD_DIM)
        q = q * (MLSTM_HEAD_DIM ** -0.5)
        log_i = i_pre.astype(f32) + mlstm_i_bias[l].astype(f32)
        log_f = jax.nn.log_sigmoid(f_pre.astype(f32) + mlstm_f_bias[l].astype(f32))
        h_t = _mlstm_chunkwise(q, k, v, log_i, log_f)
        h_t = h_t * lax.rsqrt(jnp.mean(h_t * h_t, axis=-1, keepdims=True) + RMS_EPS)
        h_t = h_t * mlstm_norm_w[l].astype(f32).reshape(MLSTM_HEADS, MLSTM_HEAD_DIM)
        h_a = jax.nn.sigmoid(o_g.astype(f32)) * h_t.reshape(bsz, s, MLSTM_WIDTH)
        y_a = (h_a * jax.nn.silu(z_a.astype(f32))).astype(x.dtype)

        pooled = _multiscale_pool(u_b)
        mixed = jnp.einsum('bsgc,gcd->bsgd', pooled, pool_w[l].astype(f32))
        mixed = mixed.reshape(bsz, s, POOL_WIDTH) * pool_scale[l].astype(f32)
        y_b = (mixed * jax.nn.silu(z_b.astype(f32))).astype(x.dtype)

        br_a = jnp.einsum('bsc,cd->bsd', y_a, w_proj_mlstm[l])
        br_b = jnp.einsum('bsc,cd->bsd', y_b, w_proj_pool[l])
        merged = jax.nn.sigmoid(g_a) * br_a + jax.nn.sigmoid(g_b) * br_b
        out = jnp.einsum('bsd,de->bse', merged, w_out[l])
        x = x + _rmsnorm(out, norm_post_w[l])
    return x
```

```python
import contextlib
import itertools
import math

import numpy as np
import concourse.bass as bass
import concourse.mybir as mybir
from concourse.bass_utils import run_bass_kernel_spmd

F32 = mybir.dt.float32
BF16 = mybir.dt.bfloat16
AF = mybir.ActivationFunctionType
ALU = mybir.AluOpType

DEBUG = False
STAGE = 9
NS_RUN = 8


class Tok:
    __slots__ = ("name", "wr", "rd", "dw", "dr")

    def __init__(self, name=""):
        self.name = name
        self.wr = {}
        self.rd = {}
        self.dw = None
        self.dr = None


class Prog:
    COMPUTE = ("pe", "act", "dve", "pool")

    def __init__(self, nc, stack, signal=None):
        self.nc = nc
        self.stack = stack
        self.eng = {"pe": nc.tensor, "act": nc.scalar, "dve": nc.vector,
                    "pool": nc.gpsimd, "sp": nc.sync}
        self.signal = signal
        self.sigset = None if signal is None else {e: set(v) for e, v in signal.items()}
        self.seq = {e: 0 for e in self.COMPUTE}
        self.cnt = {e: 0 for e in self.COMPUTE}
        self.rank = {e: {} for e in self.COMPUTE}
        self.sem = {e: stack.enter_context(nc.semaphore("s_" + e)) for e in self.COMPUTE}
        self.known = {e: {} for e in self.eng}
        self.waited = {e: set() for e in self.COMPUTE}
        self.nsem = 0
        self.ninstr = 0

    def newsem(self, name):
        self.nsem += 1
        return self.stack.enter_context(self.nc.semaphore(f"d{self.nsem}_{name}"))

    def _emit_waits(self, e, deps):
        h = self.eng[e]
        kn = self.known[e]
        for key, val in deps.items():
            if kn.get(key, 0) >= val:
                continue
            kn[key] = val
            if key[0] == "c":
                src = key[1]
                self.waited[src].add(val)
                if self.signal is None:
                    sv = val
                else:
                    sv = self.rank[src].get(val)
                    assert sv is not None, (src, val)
                h.wait_ge(self.sem[src], sv)
            else:
                h.wait_ge(key[2], val)

    def _gather(self, e, reads, writes):
        deps = {}

        def add(d):
            for k, v in d.items():
                if k[0] == "c" and k[1] == e and e == "pe":
                    continue
                if deps.get(k, 0) < v:
                    deps[k] = v
        for t in reads:
            add(t.wr)
        for t in writes:
            add(t.wr)
            add(t.rd)
        return deps

    def op(self, e, fn, reads=(), writes=()):
        deps = self._gather(e, reads, writes)
        self._emit_waits(e, deps)
        self.seq[e] += 1
        s = self.seq[e]
        ins = fn(self.eng[e])
        self.ninstr += 1
        if self.signal is None or s in self.sigset[e]:
            self.cnt[e] += 1
            self.rank[e][s] = self.cnt[e]
            ins.then_inc(self.sem[e], 1)
        key = ("c", e)
        for t in reads:
            if t.rd.get(key, 0) < s:
                t.rd[key] = s
        for t in writes:
            t.wr = {key: s}
            t.rd = {}
        return ins

    def dma(self, q, fn, reads=(), writes=(), sem_tok=None, kind=None, inc=16):
        if sem_tok is None:
            sem_tok = writes[0] if kind != "r" else reads[0]
        slot = "dw" if kind != "r" else "dr"
        st = getattr(sem_tok, slot)
        if st is None:
            st = [self.newsem(sem_tok.name + slot), 0]
            setattr(sem_tok, slot, st)
        key = ("d", id(st), st[0])
        deps = self._gather(q, reads, writes)
        deps.pop(key, None)
        self._emit_waits(q, deps)
        ins = fn(self.eng[q])
        self.ninstr += 1
        st[1] += inc
        ins.then_inc(st[0], inc)
        for t in reads:
            if t.rd.get(key, 0) < st[1]:
                t.rd[key] = st[1]
        for t in writes:
            t.wr = {key: st[1]}
            t.rd = {}
        return ins

    def wait_all(self, e, toks):
        deps = {}
        for t in toks:
            for d in (t.wr, t.rd):
                for k, v in d.items():
                    if deps.get(k, 0) < v:
                        deps[k] = v
        self._emit_waits(e, deps)

    def signals(self):
        return {e: sorted(self.waited[e]) for e in self.COMPUTE}


D = 2048
SEQ = 4096
NT1 = 32
NS1 = 8
TOK2 = 1024
HALO = 128
EPS = 1e-6
LN16 = math.log(16.0)
WIN = (2, 4, 8, 16)


def build(signal=None):
    nc = bass.Bass("TRN2", target_bir_lowering=False)
    dt_in = lambda name, shape: nc.dram_tensor(name, shape, F32, kind="ExternalInput").ap()
    xb = dt_in("xb", [SEQ, D])
    xr = dt_in("xr", [HALO + TOK2, D])
    w1 = dt_in("w1", [D, 2560])
    wif = dt_in("wif", [D, 4])
    npw_t = dt_in("npw_t", [128, 16])
    cw = dt_in("cw", [128, 32])
    cb = dt_in("cb", [128, 8])
    gbias = dt_in("gbias", [2, 2])
    nwa = dt_in("nwa", [1, 512])
    ident_d = dt_in("ident", [128, 128])
    mask_d = dt_in("mask", [128, 128])
    sel_d = dt_in("sel", [2, 256])
    small = STAGE < 3
    w2 = dt_in("w2", [D, 6144] if not small else [1, 1])
    pw = dt_in("pw", [1024, 256])
    psc_t = dt_in("psc_t", [128, 8])
    invc_d = dt_in("invc", [128, 64])
    wpm = dt_in("wpm", [D, D] if not small else [1, 1])
    wpp = dt_in("wpp", [1024, D] if not small else [1, 1])
    wout = dt_in("wout", [D, D] if not small else [1, 1])
    npost = dt_in("npost", [1, D])
    out = nc.dram_tensor("out", [TOK2, D], F32, kind="ExternalOutput").ap()
    yloc = nc.dram_tensor("yloc", [4 * 512, 1024], BF16, kind="Internal").ap()
    yall = nc.dram_tensor("yall", [4 * 4 * 512, 1024], BF16, kind="Internal").ap()
    dbg = {}
    if DEBUG:
        dbg["d_ya"] = nc.dram_tensor("d_ya", [4 * 512, 1024], BF16, kind="ExternalOutput").ap()
        dbg["d_yb"] = nc.dram_tensor("d_yb", [1024, 1024], BF16, kind="ExternalOutput").ap()
        dbg["d_mg"] = nc.dram_tensor("d_mg", [2048, 1024], BF16, kind="ExternalOutput").ap()

    with contextlib.ExitStack() as st:
        P = Prog(nc, st, signal)
        ps = lambda name, dt: st.enter_context(
            nc.psum_tensor(name, [128, 512] if dt == F32 else [128, 1024], dt))
        B0 = ps("B0", BF16); tB0 = Tok("B0")
        B1 = ps("B1", BF16); tB1 = Tok("B1")
        BA = [ps(f"BA{i}", F32) for i in range(6)]
        tBA = [Tok(f"BA{i}") for i in range(6)]

        def gsb(stack, name, shape, dt):
            return stack.enter_context(nc.sbuf_tensor(name, shape, dt))

        identb = gsb(st, "identb", [128, 128], BF16); t_identb = Tok("identb")
        identf = gsb(st, "identf", [128, 128], F32); t_identf = Tok("identf")
        npw_sb = gsb(st, "npw_sb", [128, 16], F32); t_npw = Tok("npw")
        P.dma("pool", lambda e: e.dma_start(out=identb[:], in_=ident_d), writes=[t_identb])
        P.dma("sp", lambda e: e.dma_start(out=identf[:], in_=ident_d), writes=[t_identf])
        P.dma("sp", lambda e: e.dma_start(out=npw_sb[:], in_=npw_t), writes=[t_npw])

        def norm_pre(xsrc_ap, xt, t_xt, xn, t_xn, ssq, t_ssq):
            P.dma("sp", lambda e: e.dma_start(out=xt[:], in_=xsrc_ap), writes=[t_xt])
            P.op("act", lambda e: e.activation(out=xn[:], in_=xt[:], func=AF.Square, accum_out=ssq[:, 0:1]),
                 reads=[t_xt], writes=[t_xn, t_ssq])
            P.op("act", lambda e: e.activation(out=ssq[:, 1:2], in_=ssq[:, 0:1], func=AF.Sqrt,
                                               scale=1.0 / D, bias=EPS), reads=[t_ssq], writes=[t_ssq])
            P.op("dve", lambda e: e.reciprocal(out=ssq[:, 2:3], in_=ssq[:, 1:2]), reads=[t_ssq], writes=[t_ssq])
            P.op("pool", lambda e: e.tensor_scalar(out=xn[:], in0=xt[:], scalar1=ssq[:, 2:3], scalar2=1.0,
                                                   op0=ALU.mult, op1=ALU.mult),
                 reads=[t_xt, t_ssq], writes=[t_xn])

        def norm_T_r(xn, t_xn, hT_dst8, t_hT, r):
            for kk in range(8):
                k = r * 8 + kk
                P.op("pe", lambda e: e.transpose(out=B0[:, kk * 128:(kk + 1) * 128],
                                                 in_=xn[:, k * 128:(k + 1) * 128], identity=identb[:]),
                     reads=[t_xn, t_identb], writes=[tB0])
                yield
            if r == 0:
                P.op("dve", lambda e: e.tensor_tensor(
                    out=hT_dst8(r), in0=B0[:, :].rearrange("p (c t) -> p c t", c=8),
                    in1=npw_sb[:, r * 8:(r + 1) * 8].rearrange("p (c o) -> p c o", o=1).broadcast_to([128, 8, 128]),
                    op=ALU.mult), reads=[tB0, t_npw], writes=[t_hT])
            else:
                dst8 = hT_dst8(r)
                for kk in range(8):
                    k = r * 8 + kk
                    P.op("act", lambda e: e.activation(out=dst8[:, kk, :], in_=B0[:, kk * 128:(kk + 1) * 128],
                                                       func=AF.Copy, scale=npw_sb[:, k:k + 1]),
                         reads=[tB0, t_npw], writes=[t_hT])
            yield

        def norm_T(xn, t_xn, hT_dst8, t_hT, evac_flip):
            for r in range(2):
                yield from norm_T_r(xn, t_xn, hT_dst8, t_hT, r)

        def interleave(main_gen, fill_gen, per_yield):
            fill = iter(fill_gen) if fill_gen is not None else iter(())
            for yi, _ in enumerate(main_gen):
                npy = per_yield[min(yi, len(per_yield) - 1)] if isinstance(per_yield, (list, tuple)) else per_yield
                for _i in range(npy):
                    if next(fill, "END") == "END":
                        break
            for _ in fill:
                pass

        def barrier(toks_extra=()):
            bt = []
            for e in ("act", "dve", "pool"):
                t = Tok("bar_" + e)
                if e == "act":
                    P.op(e, lambda h: h.copy(out=bar_tiles[e][:], in_=identf[0:2, 0:2]), reads=[t_identf], writes=[t])
                else:
                    P.op(e, lambda h: h.memset(bar_tiles[e][:], 0.0), writes=[t])
                bt.append(t)
            t = Tok("bar_pe")
            P.op("pe", lambda h: h.matmul(BA[0][0:2, 0:2], lhsT=identf[0:2, 0:2], rhs=identf[0:2, 0:2],
                                          start=True, stop=True), reads=[t_identf], writes=[t, tBA[0]])
            bt.append(t)
            for e in ("pe", "act", "dve", "pool", "sp"):
                P.wait_all(e, bt + list(toks_extra))

        bar_tiles = {e: gsb(st, "bar_" + e, [2, 2], F32) for e in ("act", "dve", "pool")}
        ln16_sb = gsb(st, "ln16_sb", [2, 1], F32); t_ln16 = Tok("ln16")
        P.op("dve", lambda e: e.memset(ln16_sb[:], LN16), writes=[t_ln16])

        t_yloc = Tok("yloc")
        ccsem = P.newsem("cc")

        with contextlib.ExitStack() as s1:
            sb = lambda name, shape, dt: gsb(s1, name, shape, dt)
            W1b = sb("W1b", [128, 16, 2560], BF16)
            tW1 = [Tok(f"W1_{i}") for i in range(5)]
            Wifb = sb("Wifb", [128, 16, 4], BF16); tWif = Tok("Wif")
            cw_sb = sb("cw_sb", [128, 32], F32); cb_sb = sb("cb_sb", [128, 8], F32); t_cw = Tok("cw")
            gb_sb = sb("gb_sb", [2, 2], F32); t_gb = Tok("gb")
            nwa_sb = sb("nwa_sb", [128, 512], F32); t_nwa = Tok("nwa")
            maskb = sb("maskb", [128, 128], BF16); t_mask = Tok("mask")
            sel_sb = sb("sel_sb", [2, 256], F32); t_sel = Tok("sel")
            xt = [sb(f"xt{i}", [128, D], F32) for i in range(2)]; t_xt = [Tok("xt0"), Tok("xt1")]
            xn = [sb(f"xn{i}", [128, D], BF16) for i in range(2)]; t_xn = [Tok("xn0"), Tok("xn1")]
            ssq = [sb(f"ssq{i}", [128, 4], F32) for i in range(2)]; t_ssq = [Tok("ssq0"), Tok("ssq1")]
            hT = [sb(f"hT{i}", [128, 16, 512], BF16) for i in range(2)]
            t_hT = [[Tok(f"hT{i}_{t}") for t in range(4)] for i in range(2)]
            qkp = [sb(f"qkp{i}", [128, 515], F32) for i in range(2)]; t_qkp = [Tok("qkp0"), Tok("qkp1")]
            halo3 = sb("halo3", [128, 8, 3], F32); t_halo = [Tok(f"halo{c}") for c in range(8)]
            cacc = [sb(f"cacc{i}", [128, 512], F32) for i in range(2)]; t_cacc = [Tok("cacc0"), Tok("cacc1")]
            qkT = sb("qkT", [128, 8, 512], BF16); t_qkT = [Tok(f"qkT{c}") for c in range(8)]
            zeros2 = sb("zeros2", [2, 512], F32); t_zeros = Tok("zeros2")
            g_li = sb("g_li", [2, 512], F32); g_f = sb("g_f", [2, 512], F32)
            g_nB = sb("g_nB", [2, 512], F32); g_a = sb("g_a", [2, 512], F32)
            g_M = sb("g_M", [2, 512], F32); g_t = sb("g_t", [2, 512], F32)
            g_u = sb("g_u", [2, 512], F32); g_e = sb("g_e", [2, 512], F32)
            g_sm = sb("g_sm", [2, 16], F32)
            t_g = Tok("gates")
            ucol = sb("ucol", [128, 4, 2, 2], F32); t_ucol = Tok("ucol")
            decb = sb("decb", [128, 2, 4], F32); t_decb = Tok("decb")
            vs2 = [sb(f"vs{i}", [128, 2, 257], BF16) for i in range(2)]; t_vs2 = [Tok("vs0"), Tok("vs1")]
            so2_ = [sb(f"so{i}", [128, 512], F32) for i in range(2)]; t_so2_ = [Tok("so0"), Tok("so1")]
            sz2_ = [sb(f"sz{i}", [128, 512], F32) for i in range(2)]; t_sz2_ = [Tok("sz0"), Tok("sz1")]
            gg2 = [sb(f"gg{i}", [128, 512], F32) for i in range(2)]; t_gg2 = [Tok("gg0"), Tok("gg1")]
            k_tm = sb("k_tm", [128, 512], BF16); t_ktm = Tok("ktm")
            mS2 = [sb(f"mS{i}", [128, 128], BF16) for i in range(2)]; t_mS2 = [Tok("mS0"), Tok("mS1")]
            Cst = sb("Cst", [128, 2, 2, 257], F32); t_C = [Tok("C0"), Tok("C1")]
            Cd = sb("Cd", [128, 2, 2, 257], BF16); t_Cd = [Tok("Cd0"), Tok("Cd1")]
            ep = sb("ep", [128, 2, 8], F32); t_ep = [Tok("ep0"), Tok("ep1")]
            ES = sb("ES", [128, 2], F32); t_es = [Tok("es0"), Tok("es1")]
            junk = sb("junk", [128, 256], BF16); t_junk = Tok("junk")
            ya2 = [sb(f"ya{i}", [128, 512], BF16) for i in range(2)]; t_ya2 = [Tok("ya0"), Tok("ya1")]
            yaT = [sb(f"yaT{i}", [128, 4, 128], BF16) for i in range(2)]; t_yaT = [Tok("yaT0"), Tok("yaT1")]

            for blk in range(5):
                for hf in range(2):
                    P.dma("pool", lambda e: e.dma_start(
                        out=W1b[:, hf * 8:(hf + 1) * 8, blk * 512:(blk + 1) * 512],
                        in_=w1[hf * 1024:(hf + 1) * 1024, blk * 512:(blk + 1) * 512].rearrange("(c p) n -> p c n", p=128)),
                        writes=[tW1[blk]])
                if blk == 0:
                    P.dma("pool", lambda e: e.dma_start(out=Wifb[:], in_=wif.rearrange("(c p) n -> p c n", p=128)),
                          writes=[tWif])
                    P.dma("pool", lambda e: e.dma_start(out=maskb[:], in_=mask_d), writes=[t_mask])
            P.dma("sp", lambda e: e.dma_start(out=cw_sb[:], in_=cw), writes=[t_cw])
            P.dma("sp", lambda e: e.dma_start(out=cb_sb[:], in_=cb), writes=[t_cw])
            P.dma("sp", lambda e: e.dma_start(out=gb_sb[:], in_=gbias), writes=[t_gb])
            P.dma("sp", lambda e: e.dma_start(out=nwa_sb[:], in_=nwa.partition_broadcast(128)), writes=[t_nwa])
            P.dma("sp", lambda e: e.dma_start(out=sel_sb[:], in_=sel_d), writes=[t_sel])
            P.op("dve", lambda e: e.memset(zeros2[:], 0.0), writes=[t_zeros])
            P.op("dve", lambda e: e.memset(g_sm[:], 0.0), writes=[t_g])
            P.op("pool", lambda e: e.memset(halo3[:], 0.0), writes=t_halo)

            def n1_args(tile):
                s_, tt = divmod(tile, 4)
                return s_ % 2, tt, tile % 2

            def emit_norm1_pre(tile):
                hb, tt, xi = n1_args(tile)
                norm_pre(xb[tile * 128:(tile + 1) * 128, :], xt[xi], t_xt[xi], xn[xi], t_xn[xi], ssq[xi], t_ssq[xi])

            def gen_norm1_T(tile, r=None):
                hb, tt, xi = n1_args(tile)
                dst = lambda r_: hT[hb][:, r_ * 8:(r_ + 1) * 8, tt * 128:(tt + 1) * 128]
                if r is None:
                    return norm_T(xn[xi], t_xn[xi], dst, t_hT[hb][tt], tile % 2)
                return norm_T_r(xn[xi], t_xn[xi], dst, t_hT[hb][tt], r)

            for tile in range(4):
                emit_norm1_pre(tile)
                for _ in gen_norm1_T(tile):
                    pass

            BS, tBS = BA[2], tBA[2]
            BN, tBN = BA[3], tBA[3]
            BK = [BA[4], BA[5]]; tBK = [tBA[4], tBA[5]]
            acc_i = [0]

            def next_acc(wide=False):
                pool_ = (0, 1, 3, 4, 5) if wide else (0, 1)
                i = pool_[acc_i[0] % len(pool_)]
                acc_i[0] += 1
                return BA[i], tBA[i]

            def gen_proj(s_, tt, wide=False):
                hTs = hT[s_ % 2]; t_hTs = t_hT[s_ % 2]
                pb = tt % 2
                vs = vs2[pb]; t_vs = t_vs2[pb]
                tsl = slice(tt * 128, (tt + 1) * 128)
                for blk, nm in ((2, "v"), (3, "o"), (4, "z")):
                    P_acc, t_acc = next_acc(wide)
                    for k in range(16):
                        P.op("pe", lambda e: e.matmul(P_acc[:, :], lhsT=hTs[:, k, tsl], rhs=W1b[:, k, blk * 512:(blk + 1) * 512],
                                                      start=(k == 0), stop=(k == 15)),
                             reads=[t_hTs[tt], tW1[blk]], writes=[t_acc])
                        yield
                    if nm == "v":
                        for h in range(2):
                            P.op("dve", lambda e: e.tensor_scalar(out=vs[:, h, 0:256], in0=P_acc[:, h * 256:(h + 1) * 256],
                                                                  scalar1=ucol[:, tt, 0, h:h + 1], scalar2=None, op0=ALU.mult),
                                 reads=[t_acc, t_ucol], writes=[t_vs])
                        P.op("pool", lambda e: e.tensor_copy(out=vs[:, :, 256], in_=ucol[:, tt, 0, :]),
                             reads=[t_ucol], writes=[t_vs])
                    elif nm == "o":
                        P.op("act", lambda e: e.activation(out=so2_[pb][:], in_=P_acc[:, :], func=AF.Sigmoid),
                             reads=[t_acc], writes=[t_so2_[pb]])
                    else:
                        P.op("act", lambda e: e.activation(out=sz2_[pb][:], in_=P_acc[:, :], func=AF.Silu),
                             reads=[t_acc], writes=[t_sz2_[pb]])
                P.op("pool", lambda e: e.tensor_tensor(out=gg2[pb][:], in0=so2_[pb][:], in1=sz2_[pb][:], op=ALU.mult),
                     reads=[t_so2_[pb], t_sz2_[pb]], writes=[t_gg2[pb]])
                P.op("pool", lambda e: e.tensor_tensor(out=gg2[pb][:], in0=gg2[pb][:], in1=nwa_sb[:], op=ALU.mult),
                     reads=[t_gg2[pb], t_nwa], writes=[t_gg2[pb]])
                yield

            def finish_tile(tile):
                yb_i = tile % 2
                ya = ya2[yb_i]; t_ya = t_ya2[yb_i]
                for j in range(4):
                    P.op("pe", lambda e: e.transpose(out=B1[:, 512 + j * 128:512 + (j + 1) * 128], in_=ya[:, j * 128:(j + 1) * 128],
                                                     identity=identb[:]), reads=[t_ya, t_identb], writes=[tB1])
                P.op("act", lambda e: e.copy(out=yaT[yb_i][:].rearrange("p j t -> p (j t)"), in_=B1[:, 512:1024]),
                     reads=[tB1], writes=[t_yaT[yb_i]])
                tq, to = divmod(tile, 8)
                P.dma("sp", lambda e: e.dma_start(
                    out=yloc[tq * 512:(tq + 1) * 512, to * 128:(to + 1) * 128].rearrange("(j p) n -> p j n", p=128),
                    in_=yaT[yb_i][:]), reads=[t_yaT[yb_i]], writes=[t_yloc], sem_tok=t_yaT[yb_i], kind="r")
                if tile % 8 == 7:
                    P.wait_all("pool", [t_yloc])
                    nc.gpsimd.collective_compute(
                        "AllGather", ALU.bypass, replica_groups=[[0, 1, 2, 3], [4, 5, 6, 7]],
                        ins=[yloc[tq * 512:(tq + 1) * 512, :]],
                        outs=[yall[tq * 2048:(tq + 1) * 2048, :]]).then_inc(ccsem, 1)

            def gen_mlstm(s_, tt, pre_hook=None):
                tile = s_ * 4 + tt
                pb = tt % 2
                vs = vs2[pb]; t_vs = t_vs2[pb]
                gg = gg2[pb]; t_gg = t_gg2[pb]
                ya = ya2[tile % 2]; t_ya = t_ya2[tile % 2]
                tsl = slice(tt * 128, (tt + 1) * 128)
                if tile > 0:
                    for h in range(2):
                        Ch = Cst[:, h].rearrange("p a b -> p (a b)")
                        Cdh = Cd[:, h].rearrange("p a b -> p (a b)")
                        P.op("pool", lambda e: e.tensor_scalar(out=Cdh, in0=Ch, scalar1=decb[:, h, tt:tt + 1], scalar2=1.0,
                                                               op0=ALU.mult, op1=ALU.mult),
                             reads=[t_C[h], t_decb], writes=[t_Cd[h]])
                if pre_hook is not None:
                    pre_hook()
                for h in range(2):
                    for dkc in range(2):
                        cc = 4 + 2 * h + dkc
                        P.op("pe", lambda e: e.transpose(out=B1[:, (2 * h + dkc) * 128:(2 * h + dkc + 1) * 128],
                                                         in_=qkT[:, cc, tsl], identity=identb[:]),
                             reads=[t_qkT[cc], t_identb], writes=[tB1])
                P.op("act", lambda e: e.copy(out=k_tm[:], in_=B1[:, 0:512]), reads=[tB1], writes=[t_ktm])
                yield
                for h in range(2):
                    for dkc in range(2):
                        P.op("pe", lambda e: e.matmul(BS[:, h * 256:h * 256 + 128], lhsT=qkT[:, 4 + 2 * h + dkc, tsl],
                                                      rhs=qkT[:, 2 * h + dkc, tsl], start=(dkc == 0), stop=(dkc == 1)),
                             reads=[t_qkT[4 + 2 * h + dkc], t_qkT[2 * h + dkc]], writes=[tBS])
                for h in range(2):
                    P.op("dve", lambda e: e.tensor_tensor(out=mS2[h][:], in0=BS[:, h * 256:h * 256 + 128], in1=maskb[:], op=ALU.mult),
                         reads=[tBS, t_mask], writes=[t_mS2[h]])
                yield
                for h in range(2):
                    mS = mS2[h]; t_mS = t_mS2[h]
                    P.op("pe", lambda e: e.matmul(BN[:, 0:257], lhsT=mS[:], rhs=vs[:, h, :], start=True, stop=(tile == 0)),
                         reads=[t_mS, t_vs], writes=[tBN])
                    if tile > 0:
                        for dkc in range(2):
                            P.op("pe", lambda e: e.matmul(BN[:, 0:257], lhsT=qkT[:, 2 * h + dkc, tsl], rhs=Cd[:, h, dkc, :],
                                                          start=False, stop=(dkc == 1)),
                                 reads=[t_qkT[2 * h + dkc], t_Cd[h]], writes=[tBN])
                    for dkc in range(2):
                        P.op("pe", lambda e: e.matmul(BK[dkc][:, 0:257], lhsT=k_tm[:, (2 * h + dkc) * 128:(2 * h + dkc + 1) * 128],
                                                      rhs=vs[:, h, :], start=True, stop=True),
                             reads=[t_ktm, t_vs], writes=[tBK[dkc]])
                        if tile == 0:
                            P.op("dve", lambda e: e.tensor_copy(out=Cst[:, h, dkc, :], in_=BK[dkc][:, 0:257]),
                                 reads=[tBK[dkc]], writes=[t_C[h]])
                        else:
                            P.op("dve", lambda e: e.scalar_tensor_tensor(out=Cst[:, h, dkc, :], in0=Cst[:, h, dkc, :],
                                                                         scalar=decb[:, h, tt:tt + 1], in1=BK[dkc][:, 0:257],
                                                                         op0=ALU.mult, op1=ALU.add),
                                 reads=[tBK[dkc], t_decb, t_Cd[h]], writes=[t_C[h]])
                    E = ep[:, h]
                    te = t_ep[h]
                    P.op("act", lambda e: e.activation(out=junk[:], in_=BN[:, 0:256], func=AF.Square, scale=1.0 / 16,
                                                       accum_out=ES[:, h:h + 1]),
                         reads=[tBN], writes=[t_junk, t_es[h]])
                    P.op("dve", lambda e: e.tensor_tensor(out=E[:, 0:1], in0=BN[:, 256:257], in1=ucol[:, tt, 1, h:h + 1],
                                                          op=ALU.max), reads=[tBN, t_ucol], writes=[te])
                    P.op("dve", lambda e: e.scalar_tensor_tensor(out=E[:, 1:2], in0=BN[:, 256:257], scalar=-1.0, in1=E[:, 0:1],
                                                                 op0=ALU.mult, op1=ALU.max), reads=[tBN, te], writes=[te])
                    P.op("dve", lambda e: e.scalar_tensor_tensor(out=E[:, 3:4], in0=E[:, 1:2], scalar=EPS, in1=E[:, 1:2],
                                                                 op0=ALU.mult, op1=ALU.mult), reads=[te], writes=[te])
                    P.op("act", lambda e: e.activation(out=E[:, 4:5], in_=E[:, 3:4], func=AF.Sqrt, bias=ES[:, h:h + 1], scale=1.0),
                         reads=[te, t_es[h]], writes=[te])
                    P.op("dve", lambda e: e.reciprocal(out=E[:, 6:7], in_=E[:, 4:5]), reads=[te], writes=[te])
                    P.op("dve", lambda e: e.scalar_tensor_tensor(out=ya[:, h * 256:(h + 1) * 256], in0=BN[:, 0:256],
                                                                 scalar=E[:, 6:7], in1=gg[:, h * 256:(h + 1) * 256],
                                                                 op0=ALU.mult, op1=ALU.mult),
                         reads=[tBN, te, t_gg], writes=[t_ya])
                    if h == 0 and tile > 0:
                        finish_tile(tile - 1)
                    yield

            def chain(*gens):
                for g in gens:
                    if g is not None:
                        yield from g

            for s_ in range(NS_RUN):
                hb = s_ % 2
                hTs = hT[hb]
                t_hTs = t_hT[hb]
                for gi in range(2):
                    P_acc, t_acc = BS, tBS
                    for k in range(16):
                        P.op("pe", lambda e: e.matmul(P_acc[0:2, 0:512],
                                                      lhsT=Wifb[:, k, 2 * gi:2 * gi + 2], rhs=hTs[:, k, :],
                                                      start=(k == 0), stop=(k == 15)),
                             reads=t_hTs + [tWif], writes=[t_acc])
                    dst = g_li if gi == 0 else g_f
                    P.op("act", lambda e: e.activation(out=dst[:], in_=P_acc[0:2, 0:512], func=AF.Identity,
                                                       bias=gb_sb[:, gi:gi + 1]),
                         reads=[t_acc, t_gb], writes=[t_g])
                G = lambda eng, fn: P.op(eng, fn, reads=[t_g, t_zeros], writes=[t_g])
                G("act", lambda e: e.activation(out=g_f[:], in_=g_f[:], func=AF.Exp, scale=-1.0))
                G("act", lambda e: e.activation(out=g_f[:], in_=g_f[:], func=AF.Ln, bias=1.0))
                G("dve", lambda e: e.tensor_tensor_scan(out=g_nB[:], data0=g_f[:], data1=zeros2[:],
                                                        initial=g_sm[:, 0:1], op0=ALU.add, op1=ALU.add))
                G("dve", lambda e: e.tensor_tensor(out=g_a[:], in0=g_li[:], in1=g_nB[:], op=ALU.add))
                G("dve", lambda e: e.tensor_tensor_scan(out=g_M[:], data0=g_a[:], data1=zeros2[:],
                                                        initial=g_sm[:, 1:2], op0=ALU.max, op1=ALU.add))
                M3 = g_M[:].rearrange("p (c t) -> p c t", t=128)
                Mend = M3[:, :, 127:128]
                G("dve", lambda e: e.tensor_copy(out=g_sm[:, 4:5], in_=g_sm[:, 1:2]))
                G("dve", lambda e: e.tensor_copy(out=g_sm[:, 5:8], in_=g_M[:, 127:127 + 3 * 128:128]))
                G("dve", lambda e: e.tensor_tensor(out=g_sm[:, 8:12], in0=g_sm[:, 4:8], in1=g_M[:, 127::128],
                                                   op=ALU.subtract))
                G("act", lambda e: e.activation(out=g_sm[:, 12:16], in_=g_sm[:, 8:12], func=AF.Exp))
                G("dve", lambda e: e.tensor_copy(out=g_sm[:, 0:1], in_=g_nB[:, 511:512]))
                G("dve", lambda e: e.tensor_copy(out=g_sm[:, 1:2], in_=g_M[:, 511:512]))
                G("dve", lambda e: e.tensor_tensor(out=g_t[:].rearrange("p (c t) -> p c t", t=128),
                                                   in0=g_a[:].rearrange("p (c t) -> p c t", t=128),
                                                   in1=Mend.broadcast_to([2, 4, 128]), op=ALU.subtract))
                G("act", lambda e: e.activation(out=g_u[:], in_=g_t[:], func=AF.Exp))
                G("dve", lambda e: e.tensor_tensor(out=g_a[:].rearrange("p (c t) -> p c t", t=128),
                                                   in0=g_nB[:].rearrange("p (c t) -> p c t", t=128),
                                                   in1=Mend.broadcast_to([2, 4, 128]), op=ALU.subtract))
                G("act", lambda e: e.activation(out=g_e[:], in_=g_a[:], func=AF.Exp, bias=ln16_sb[:, 0:1]))
                for cc in range(8):
                    P_acc, t_acc = next_acc(wide=True)
                    for k in range(16):
                        P.op("pe", lambda e: e.matmul(P_acc[:, :], lhsT=W1b[:, k, cc * 128:(cc + 1) * 128], rhs=hTs[:, k, :],
                                                      start=(k == 0), stop=(k == 15)),
                             reads=t_hTs + [tW1[cc // 4]], writes=[t_acc])
                    qp = qkp[cc % 2]; t_qp = t_qkp[cc % 2]
                    P.op("pool", lambda e: e.tensor_copy(out=qp[:, 0:3], in_=halo3[:, cc, :]),
                         reads=[t_halo[cc]], writes=[t_qp])
                    P.op("act", lambda e: e.copy(out=qp[:, 3:515], in_=P_acc[:, :]),
                         reads=[t_acc], writes=[t_qp])
                    P.op("pool", lambda e: e.tensor_copy(out=halo3[:, cc, :], in_=qp[:, 512:515]),
                         reads=[t_qp], writes=[t_halo[cc]])
                    ca = cacc[cc % 2]; t_ca = t_cacc[cc % 2]
                    P.op("dve", lambda e: e.tensor_scalar(out=ca[:], in0=qp[:, 0:512],
                                                          scalar1=cw_sb[:, cc * 4:cc * 4 + 1], scalar2=cb_sb[:, cc:cc + 1],
                                                          op0=ALU.mult, op1=ALU.add),
                         reads=[t_qp, t_cw], writes=[t_ca])
                    for j in range(1, 4):
                        P.op("dve", lambda e: e.scalar_tensor_tensor(out=ca[:], in0=qp[:, j:j + 512],
                                                                     scalar=cw_sb[:, cc * 4 + j:cc * 4 + j + 1], in1=ca[:],
                                                                     op0=ALU.mult, op1=ALU.add),
                             reads=[t_qp, t_cw], writes=[t_ca])
                    P.op("act", lambda e: e.activation(out=qkT[:, cc, :], in_=ca[:], func=AF.Silu),
                         reads=[t_ca], writes=[t_qkT[cc]])
                for c in range(4):
                    for qi, src in enumerate((g_u, g_e)):
                        col = 128 + (c * 2 + qi) * 2
                        P.op("pe", lambda e: e.transpose(out=BS[:, col:col + 2], in_=src[:, c * 128:(c + 1) * 128],
                                                         identity=identf[0:2, 0:2]),
                             reads=[t_g, t_identf], writes=[tBS])
                for h in range(2):
                    P.op("pe", lambda e: e.matmul(BS[:, 160 + 4 * h:164 + 4 * h], lhsT=sel_sb[:, h * 128:(h + 1) * 128],
                                                  rhs=g_sm[:, 12:16], start=True, stop=True),
                         reads=[t_g, t_sel], writes=[tBS])
                P.op("dve", lambda e: e.tensor_copy(out=ucol[:].rearrange("p c q h -> p (c q h)"), in_=BS[:, 128:144]),
                     reads=[tBS], writes=[t_ucol])
                P.op("dve", lambda e: e.tensor_copy(out=decb[:].rearrange("p h c -> p (h c)"), in_=BS[:, 160:168]),
                     reads=[tBS], writes=[t_decb])
                for _ in gen_proj(s_, 0, wide=True):
                    pass
                for tt in range(4):
                    tile = s_ * 4 + tt
                    nxt = tile + 4 < NS_RUN * 4
                    if tile == 0 and nxt:
                        emit_norm1_pre(4)
                    hook = (lambda t5=tile + 5: emit_norm1_pre(t5)) if tile + 5 < NS_RUN * 4 else None
                    pj = gen_proj(s_, tt + 1) if tt < 3 else iter(())
                    fill = chain(itertools.islice(pj, 24),
                                 gen_norm1_T(tile + 4, 0) if nxt else None,
                                 pj,
                                 gen_norm1_T(tile + 4, 1) if nxt else None)
                    interleave(gen_mlstm(s_, tt, hook), fill, [10, 8, 30, 100])
            finish_tile(NS_RUN * 4 - 1)
            barrier([t_yloc])
        if STAGE == 1:
            t_dd = Tok("dd")
            P.dma("sp", lambda e: e.dma_start(out=out[0:128, 0:128], in_=identf[:]), reads=[t_identf], kind="r", sem_tok=t_dd)
            if DEBUG:
                P.dma("sp", lambda e: e.dma_start(out=dbg["d_ya"], in_=yloc), reads=[t_yloc], kind="r", sem_tok=t_dd)
            P.wait_all("sp", [t_identf, t_yloc])
            return nc, P.signals(), P.ninstr
        if STAGE == 2:
            with contextlib.ExitStack() as s2x:
                yaTr = gsb(s2x, "yaTr", [128, 16, TOK2], BF16); t_yaTr = Tok("yaTr")
                nc.sync.wait_ge(ccsem, 4)
                pid = nc.sync.partition_id()
                tqv = pid % 4
                for r in range(4):
                    P.dma("sp", lambda e: e.dma_start(
                        out=yaTr[:, r * 4:(r + 1) * 4, :],
                        in_=yall[bass.ts(tqv * 4 + r, 512), :].rearrange("(j p) n -> p j n", p=128)),
                        writes=[t_yaTr])
                P.dma("sp", lambda e: e.dma_start(out=dbg["d_ya"].rearrange("(c p) n -> p c n", p=128), in_=yaTr[:]),
                      reads=[t_yaTr], kind="r", sem_tok=t_yaTr)
                P.wait_all("sp", [t_yaTr])
                nc.gpsimd.wait_ge(ccsem, 4)
            return nc, P.signals(), P.ninstr
        with contextlib.ExitStack() as s2:
            sb = lambda name, shape, dt: gsb(s2, name, shape, dt)
            hT2 = sb("hT2", [128, 16, HALO + TOK2], BF16); t_hT2 = [Tok(f"hT2_{i}") for i in range(9)]
            ybT = sb("ybT", [128, 8, TOK2], BF16); t_ybT = [Tok(f"ybT{i}") for i in range(8)]
            psc_sb = sb("psc_sb", [128, 8], F32); t_psc = Tok("psc")
            P.dma("sp", lambda e: e.dma_start(out=psc_sb[:], in_=psc_t), writes=[t_psc])
            acc2 = [0]

            def next_acc2():
                i = acc2[0] % 6
                acc2[0] += 1
                return BA[i], tBA[i]

            with contextlib.ExitStack() as s2a:
                sa = lambda name, shape, dt: gsb(s2a, name, shape, dt)
                xt2 = [sa(f"x2t{i}", [128, D], F32) for i in range(2)]; t_xt2 = [Tok("x2t0"), Tok("x2t1")]
                xn2 = [sa(f"x2n{i}", [128, D], BF16) for i in range(2)]; t_xn2 = [Tok("x2n0"), Tok("x2n1")]
                ssq2 = [sa(f"ssq2{i}", [128, 4], F32) for i in range(2)]; t_ssq2 = [Tok("ssq20"), Tok("ssq21")]
                def pre2(tile):
                    i = tile % 2
                    norm_pre(xr[tile * 128:(tile + 1) * 128, :], xt2[i], t_xt2[i], xn2[i], t_xn2[i], ssq2[i], t_ssq2[i])
                pre2(0)
                for tile in range(9):
                    i = tile % 2
                    if tile + 1 < 9:
                        pre2(tile + 1)
                    for _ in norm_T(xn2[i], t_xn2[i], lambda r: hT2[:, r * 8:(r + 1) * 8, tile * 128:(tile + 1) * 128],
                                    t_hT2[tile], tile % 2):
                        pass
                Wb = [sa(f"Wb{i}", [128, 16, 512], BF16) for i in range(2)]; t_Wb = [Tok("Wb0"), Tok("Wb1")]
                uT = sa("uT", [128, 8, HALO + TOK2], F32); t_uT = [Tok(f"uT{i}") for i in range(8)]
                szb = sa("szb", [128, 8, TOK2], BF16); t_szb = [Tok(f"szb{i}") for i in range(8)]
                plT = sa("plT", [128, 8, TOK2], BF16); t_plT = [Tok(f"plT{i}") for i in range(8)]
                sA = sa("sA", [128, HALO + TOK2], F32); sB = sa("sB", [128, HALO + TOK2], F32)
                t_sA = Tok("sA"); t_sB = Tok("sB")
                pwb = sa("pwb", [128, 8, 256], BF16); t_pwb = Tok("pwb")
                invc = sa("invc_sb", [128, 64], F32); t_invc = Tok("invc")
                tmp16 = sa("tmp16", [128, 16], F32); t_tmp16 = Tok("tmp16")
                P.dma("pool", lambda e: e.dma_start(out=pwb[:], in_=pw.rearrange("(c p) n -> p c n", p=128)), writes=[t_pwb])
                P.dma("sp", lambda e: e.dma_start(out=invc[:], in_=invc_d), writes=[t_invc])
                def load_wb(wb):
                    W = Wb[wb % 2]; tW = t_Wb[wb % 2]
                    for hf in range(2):
                        P.dma("pool", lambda e: e.dma_start(
                            out=W[:, hf * 8:(hf + 1) * 8, :],
                            in_=w2[hf * 1024:(hf + 1) * 1024, wb * 512:(wb + 1) * 512].rearrange("(c p) n -> p c n", p=128)),
                            writes=[tW])
                load_wb(0)
                for wb in range(4):
                    W = Wb[wb % 2]; tW = t_Wb[wb % 2]
                    if wb + 1 < 4:
                        load_wb(wb + 1)
                    for ci in range(4):
                        cc = wb * 4 + ci
                        for n in range(3 if cc < 8 else 2):
                            lo, hi = (HALO + n * 512, HALO + (n + 1) * 512) if n < 2 else (0, HALO)
                            P_acc, t_acc = next_acc2()
                            for k in range(16):
                                P.op("pe", lambda e: e.matmul(P_acc[:, 0:hi - lo], lhsT=W[:, k, ci * 128:(ci + 1) * 128],
                                                              rhs=hT2[:, k, lo:hi], start=(k == 0), stop=(k == 15)),
                                     reads=t_hT2 + [tW], writes=[t_acc])
                            if cc < 8:
                                P.op("act", lambda e: e.copy(out=uT[:, cc, lo:hi], in_=P_acc[:, 0:hi - lo]),
                                     reads=[t_acc], writes=[t_uT[cc]])
                            else:
                                P.op("act", lambda e: e.activation(out=szb[:, cc - 8, lo - HALO:hi - HALO], in_=P_acc[:, :],
                                                                   func=AF.Silu), reads=[t_acc], writes=[t_szb[cc - 8]])
                        if cc < 8:
                            g = cc // 2
                            u_ = uT[:, cc, :]
                            cur, t_cur = u_, t_uT[cc]
                            bufs = [(sA, t_sA), (sB, t_sB)]
                            sh = 1
                            NB = HALO + TOK2
                            for step in range(g + 1):
                                dst, t_dst = bufs[step % 2]
                                P.op("dve", lambda e: e.tensor_tensor(out=dst[:, 64:NB], in0=cur[:, 64:NB],
                                                                      in1=cur[:, 64 - sh:NB - sh], op=ALU.add),
                                     reads=[t_cur], writes=[t_dst])
                                cur, t_cur = dst, t_dst
                                sh *= 2
                            src, t_src = cur, t_cur
                            w_ = float(WIN[g])
                            P.op("dve", lambda e: e.scalar_tensor_tensor(out=plT[:, cc, :], in0=src[:, HALO:], scalar=1.0 / w_,
                                                                         in1=u_[:, HALO:], op0=ALU.mult, op1=ALU.subtract),
                                 reads=[t_src, t_uT[cc]], writes=[t_plT[cc]])
                            P.op("dve", lambda e: e.tensor_tensor(out=tmp16[:], in0=src[:, HALO:HALO + 16],
                                                                  in1=invc[:, g * 16:(g + 1) * 16], op=ALU.mult),
                                 reads=[t_src, t_invc], writes=[t_tmp16])
                            P.op("dve", lambda e: e.tensor_tensor(out=plT[:, cc, 0:16], in0=tmp16[:], in1=u_[:, HALO:HALO + 16],
                                                                  op=ALU.subtract),
                                 reads=[t_tmp16, t_uT[cc]], writes=[t_plT[cc]])
                for g in range(4):
                    for dc in range(2):
                        ch = g * 2 + dc
                        for n in range(2):
                            P_acc, t_acc = next_acc2()
                            for kc in range(2):
                                P.op("pe", lambda e: e.matmul(P_acc[:, :], lhsT=pwb[:, g * 2 + kc, dc * 128:(dc + 1) * 128],
                                                              rhs=plT[:, g * 2 + kc, n * 512:(n + 1) * 512],
                                                              start=(kc == 0), stop=(kc == 1)),
                                     reads=[t_pwb, t_plT[g * 2 + kc]], writes=[t_acc])
                            P.op("dve", lambda e: e.scalar_tensor_tensor(out=ybT[:, ch, n * 512:(n + 1) * 512], in0=P_acc[:, :],
                                                                         scalar=psc_sb[:, ch:ch + 1],
                                                                         in1=szb[:, ch, n * 512:(n + 1) * 512],
                                                                         op0=ALU.mult, op1=ALU.mult),
                                 reads=[t_acc, t_psc, t_szb[ch]], writes=[t_ybT[ch]])
                if DEBUG:
                    P.dma("sp", lambda e: e.dma_start(out=dbg["d_yb"].rearrange("(c p) n -> p c n", p=128), in_=ybT[:]),
                          reads=t_ybT, kind="r", sem_tok=t_ybT[0])
                barrier(t_ybT)
            mgT = sb("mgT", [128, 16, TOK2], BF16); t_mgT = [Tok(f"mgT{i}") for i in range(16)]
            with contextlib.ExitStack() as s2b:
                sa = lambda name, shape, dt: gsb(s2b, name, shape, dt)
                yaTr = sa("yaTr", [128, 16, TOK2], BF16); t_yaTr = Tok("yaTr")
                WG = [[sa(f"WG{i}_{j}", [128, 16 if j < 3 else 8, 256], BF16) for j in range(4)] for i in range(2)]
                t_WG = [[Tok(f"WG{i}_{j}") for j in range(4)] for i in range(2)]
                sg = [sa(f"sg{i}", [128, 512], F32) for i in range(2)]; t_sg = [Tok("sg0"), Tok("sg1")]
                m1 = [sa(f"m1{i}", [128, 512], F32) for i in range(2)]; t_m1 = [Tok("m10"), Tok("m11")]
                nc.sync.wait_ge(ccsem, 4)
                pid = nc.sync.partition_id()
                tqv = pid % 4
                for r in range(4):
                    P.dma("sp", lambda e: e.dma_start(
                        out=yaTr[:, r * 4:(r + 1) * 4, :],
                        in_=yall[bass.ts(tqv * 4 + r, 512), :].rearrange("(j p) n -> p j n", p=128)),
                        writes=[t_yaTr])
                if DEBUG:
                    P.dma("sp", lambda e: e.dma_start(out=dbg["d_ya"].rearrange("(c p) n -> p c n", p=128), in_=yaTr[:]),
                          reads=[t_yaTr], kind="r", sem_tok=t_yaTr)
                srcs = [(w2, 2048, 16), (w2, 4096, 16), (wpm, 0, 16), (wpp, 0, 8)]
                def load_wg(dg):
                    bi = dg % 2
                    for j, (wsrc, coff, nk) in enumerate(srcs):
                        P.dma("pool", lambda e: e.dma_start(
                            out=WG[bi][j][:],
                            in_=wsrc[0:nk * 128, coff + dg * 256:coff + (dg + 1) * 256].rearrange("(c p) n -> p c n", p=128)),
                            writes=[t_WG[bi][j]])
                load_wg(0)
                for dg in range(8):
                    bi = dg % 2
                    if dg + 1 < 8:
                        load_wg(dg + 1)
                    for di in range(2):
                        dmc = dg * 2 + di
                        dsl = slice(di * 128, (di + 1) * 128)
                        for n in range(2):
                            nsl = slice(n * 512, (n + 1) * 512)
                            hsl = slice(HALO + n * 512, HALO + (n + 1) * 512)
                            for br in range(2):
                                P_g, t_g2 = next_acc2()
                                for k in range(16):
                                    P.op("pe", lambda e: e.matmul(P_g[:, :], lhsT=WG[bi][br][:, k, dsl], rhs=hT2[:, k, hsl],
                                                                  start=(k == 0), stop=(k == 15)),
                                         reads=t_hT2 + [t_WG[bi][br]], writes=[t_g2])
                                P.op("act", lambda e: e.activation(out=sg[br][:], in_=P_g[:, :], func=AF.Sigmoid),
                                     reads=[t_g2], writes=[t_sg[br]])
                                P_b, t_b = next_acc2()
                                nk = 16 if br == 0 else 8
                                actT = yaTr if br == 0 else ybT
                                rds = [t_yaTr] if br == 0 else t_ybT
                                for k in range(nk):
                                    P.op("pe", lambda e: e.matmul(P_b[:, :], lhsT=WG[bi][2 + br][:, k, dsl], rhs=actT[:, k, nsl],
                                                                  start=(k == 0), stop=(k == nk - 1)),
                                         reads=rds + [t_WG[bi][2 + br]], writes=[t_b])
                                P.op("dve", lambda e: e.tensor_tensor(out=m1[br][:], in0=P_b[:, :], in1=sg[br][:], op=ALU.mult),
                                     reads=[t_b, t_sg[br]], writes=[t_m1[br]])
                            P.op("dve", lambda e: e.tensor_tensor(out=mgT[:, dmc, nsl], in0=m1[0][:], in1=m1[1][:], op=ALU.add),
                                 reads=t_m1, writes=[t_mgT[dmc]])
                if DEBUG:
                    P.dma("sp", lambda e: e.dma_start(out=dbg["d_mg"].rearrange("(c p) n -> p c n", p=128), in_=mgT[:]),
                          reads=t_mgT, kind="r", sem_tok=t_mgT[0])
                barrier(t_mgT)
            with contextlib.ExitStack() as s2c:
                sa = lambda name, shape, dt: gsb(s2c, name, shape, dt)
                Wo = sa("Wo", [128, 16, D], BF16); t_Wo = [Tok(f"Wo{i}") for i in range(4)]
                npo = sa("npo", [128, D], F32); t_npo = Tok("npo")
                ob = [sa(f"ob{i}", [128, D], F32) for i in range(2)]; t_ob = [Tok("ob0"), Tok("ob1")]
                jk = sa("jk", [128, D], BF16); t_jk = Tok("jk")
                xres = [sa(f"xres{i}", [128, D], F32) for i in range(2)]; t_xres = [Tok("xres0"), Tok("xres1")]
                so2 = [sa(f"so2{i}", [128, 4], F32) for i in range(2)]; t_so2 = [Tok("so20"), Tok("so21")]
                for cbk in range(4):
                    for hf in range(2):
                        P.dma("pool", lambda e: e.dma_start(
                            out=Wo[:, hf * 8:(hf + 1) * 8, cbk * 512:(cbk + 1) * 512],
                            in_=wout[hf * 1024:(hf + 1) * 1024, cbk * 512:(cbk + 1) * 512].rearrange("(c p) n -> p c n", p=128)),
                            writes=[t_Wo[cbk]])
                P.dma("sp", lambda e: e.dma_start(out=npo[:], in_=npost.partition_broadcast(128)), writes=[t_npo])
                for tt in range(8):
                    i = tt % 2
                    tsl = slice(tt * 128, (tt + 1) * 128)
                    P.dma("sp", lambda e: e.dma_start(out=xres[i][:], in_=xr[HALO + tt * 128:HALO + (tt + 1) * 128, :]),
                          writes=[t_xres[i]])
                    for cbk in range(4):
                        P_acc, t_acc = next_acc2()
                        for k in range(16):
                            P.op("pe", lambda e: e.matmul(P_acc[:, :], lhsT=mgT[:, k, tsl], rhs=Wo[:, k, cbk * 512:(cbk + 1) * 512],
                                                          start=(k == 0), stop=(k == 15)),
                                 reads=[t_mgT[k], t_Wo[cbk]], writes=[t_acc])
                        if cbk % 2 == 0:
                            P.op("act", lambda e: e.copy(out=ob[i][:, cbk * 512:(cbk + 1) * 512], in_=P_acc[:, :]),
                                 reads=[t_acc], writes=[t_ob[i]])
                        else:
                            P.op("dve", lambda e: e.tensor_copy(out=ob[i][:, cbk * 512:(cbk + 1) * 512], in_=P_acc[:, :]),
                                 reads=[t_acc], writes=[t_ob[i]])
                    S2 = so2[i]; tS = t_so2[i]
                    P.op("act", lambda e: e.activation(out=jk[:], in_=ob[i][:], func=AF.Square, accum_out=S2[:, 0:1]),
                         reads=[t_ob[i]], writes=[t_jk, tS])
                    P.op("act", lambda e: e.activation(out=S2[:, 1:2], in_=S2[:, 0:1], func=AF.Sqrt, scale=1.0 / D, bias=EPS),
                         reads=[tS], writes=[tS])
                    P.op("dve", lambda e: e.reciprocal(out=S2[:, 2:3], in_=S2[:, 1:2]), reads=[tS], writes=[tS])
                    P.op("dve", lambda e: e.scalar_tensor_tensor(out=ob[i][:], in0=ob[i][:], scalar=S2[:, 2:3], in1=npo[:],
                                                                 op0=ALU.mult, op1=ALU.mult),
                         reads=[tS, t_npo], writes=[t_ob[i]])
                    P.op("pool", lambda e: e.tensor_tensor(out=ob[i][:], in0=ob[i][:], in1=xres[i][:], op=ALU.add),
                         reads=[t_xres[i]], writes=[t_ob[i]])
                    P.dma("sp", lambda e: e.dma_start(out=out[tsl, :], in_=ob[i][:]), reads=[t_ob[i]], kind="r")
                P.wait_all("sp", t_ob)
                nc.gpsimd.wait_ge(ccsem, 4)
        return nc, P.signals(), P.ninstr


_CACHE = {}


def _get_nc():
    if "nc" not in _CACHE:
        _, sig, _ = build(None)
        nc, _, n = build(sig)
        _CACHE["nc"] = nc
    return _CACHE["nc"]


def _prep_inputs(x, norm_pre_w, w_in, mlstm_i_bias, mlstm_f_bias, qk_conv_w, qk_conv_b,
                 mlstm_norm_w, pool_w, pool_scale, w_proj_mlstm, w_proj_pool, w_out, norm_post_w):
    f = lambda a: np.ascontiguousarray(np.asarray(a, dtype=np.float32))
    x = f(x); w_in0 = f(w_in)[0]
    ident = np.eye(128, dtype=np.float32)
    mask = np.triu(np.ones((128, 128), np.float32))
    sel = np.zeros((2, 256), np.float32); sel[0, 0:128] = 1.0; sel[1, 128:256] = 1.0
    npw_t = f(f(norm_pre_w)[0].reshape(16, 128).T)
    w2 = f(w_in0[:, 10256:16400])
    pw = f(f(pool_w)[0].reshape(1024, 256))
    psc_t = f(f(pool_scale)[0].reshape(8, 128).T)
    wpm = f(f(w_proj_mlstm)[0]); wpp = f(f(w_proj_pool)[0]); wo = f(f(w_out)[0])
    npost = f(f(norm_post_w)[0].reshape(1, D))
    cwf = f(qk_conv_w)[0]; cbf = f(qk_conv_b)[0]
    in_maps = []
    for c in range(8):
        b, j = divmod(c, 4)
        heads = (2 * j, 2 * j + 1)
        cols = []
        for base in (0, 2048, 4096, 6144, 8192):
            for h in heads:
                cols.append(np.arange(base + h * 256, base + (h + 1) * 256))
        cols = np.concatenate(cols)
        w1 = f(w_in0[:, cols])
        wif = f(w_in0[:, [10240 + heads[0], 10240 + heads[1], 10248 + heads[0], 10248 + heads[1]]])
        qkcols = cols[:1024]
        cw_c = cwf[:, qkcols]
        cb_c = cbf[qkcols]
        cw_l = f(cw_c.reshape(4, 8, 128).transpose(2, 1, 0).reshape(128, 32))
        cb_l = f(cb_c.reshape(8, 128).T)
        gbias = f(np.stack([f(mlstm_i_bias)[0][list(heads)], f(mlstm_f_bias)[0][list(heads)]], axis=1))
        nwa = f(f(mlstm_norm_w)[0][heads[0] * 256:(heads[1] + 1) * 256].reshape(1, 512))
        xr = np.zeros((HALO + TOK2, D), np.float32)
        xr[HALO:] = x[b, j * TOK2:(j + 1) * TOK2]
        if j > 0:
            xr[:HALO] = x[b, j * TOK2 - HALO:j * TOK2]
        invc = np.zeros((4, 16), np.float32)
        for g in range(4):
            for t in range(16):
                invc[g, t] = 1.0 / (min(t + 1, WIN[g]) if j == 0 else WIN[g])
        invc = f(np.broadcast_to(invc.reshape(1, 64), (128, 64)))
        in_maps.append(dict(xb=f(x[b]), xr=xr, w1=w1, wif=wif, npw_t=npw_t, cw=cw_l, cb=cb_l, gbias=gbias,
                            nwa=nwa, ident=ident, mask=mask, sel=sel, w2=w2, pw=pw, psc_t=psc_t, invc=invc,
                            wpm=wpm, wpp=wpp, wout=wo, npost=npost))
    return in_maps


def kernel(**inputs):
    in_maps = _prep_inputs(**inputs)
    nc = _get_nc()
    if STAGE < 3:
        for m in in_maps:
            for k in ("w2", "wpm", "wpp", "wout"):
                m[k] = np.zeros((1, 1), np.float32)
    res = run_bass_kernel_spmd(nc, in_maps, core_ids=list(range(8)))
    outp = np.zeros((2, SEQ, D), np.float32)
    for c in range(8):
        b, j = divmod(c, 4)
        outp[b, j * TOK2:(j + 1) * TOK2] = res.results[c]["out"]
    if DEBUG:
        kernel.debug = [res.results[c] for c in range(8)]
    return outp
```

```python
import contextlib
import itertools
import math

import numpy as np
import concourse.bass as bass
import concourse.mybir as mybir
from concourse.bass_utils import run_bass_kernel_spmd

F32 = mybir.dt.float32
BF16 = mybir.dt.bfloat16
AF = mybir.ActivationFunctionType
ALU = mybir.AluOpType

DEBUG = False
STAGE = 9
NS_RUN = 8


class Tok:
    __slots__ = ("name", "wr", "rd", "dw", "dr")

    def __init__(self, name=""):
        self.name = name
        self.wr = {}
        self.rd = {}
        self.dw = None
        self.dr = None


class Prog:
    COMPUTE = ("pe", "act", "dve", "pool")

    def __init__(self, nc, stack, signal=None):
        self.nc = nc
        self.stack = stack
        self.eng = {"pe": nc.tensor, "act": nc.scalar, "dve": nc.vector,
                    "pool": nc.gpsimd, "sp": nc.sync}
        self.signal = signal
        self.sigset = None if signal is None else {e: set(v) for e, v in signal.items()}
        self.seq = {e: 0 for e in self.COMPUTE}
        self.cnt = {e: 0 for e in self.COMPUTE}
        self.rank = {e: {} for e in self.COMPUTE}
        self.sem = {e: stack.enter_context(nc.semaphore("s_" + e)) for e in self.COMPUTE}
        self.known = {e: {} for e in self.eng}
        self.waited = {e: set() for e in self.COMPUTE}
        self.nsem = 0
        self.ninstr = 0

    def newsem(self, name):
        self.nsem += 1
        return self.stack.enter_context(self.nc.semaphore(f"d{self.nsem}_{name}"))

    def _emit_waits(self, e, deps):
        h = self.eng[e]
        kn = self.known[e]
        for key, val in deps.items():
            if kn.get(key, 0) >= val:
                continue
            kn[key] = val
            if key[0] == "c":
                src = key[1]
                self.waited[src].add(val)
                if self.signal is None:
                    sv = val
                else:
                    sv = self.rank[src].get(val)
                    assert sv is not None, (src, val)
                h.wait_ge(self.sem[src], sv)
            else:
                h.wait_ge(key[2], val)

    def _gather(self, e, reads, writes):
        deps = {}

        def add(d):
            for k, v in d.items():
                if k[0] == "c" and k[1] == e and e == "pe":
                    continue
                if deps.get(k, 0) < v:
                    deps[k] = v
        for t in reads:
            add(t.wr)
        for t in writes:
            add(t.wr)
            add(t.rd)
        return deps

    def op(self, e, fn, reads=(), writes=()):
        deps = self._gather(e, reads, writes)
        self._emit_waits(e, deps)
        self.seq[e] += 1
        s = self.seq[e]
        ins = fn(self.eng[e])
        self.ninstr += 1
        if self.signal is None or s in self.sigset[e]:
            self.cnt[e] += 1
            self.rank[e][s] = self.cnt[e]
            ins.then_inc(self.sem[e], 1)
        key = ("c", e)
        for t in reads:
            if t.rd.get(key, 0) < s:
                t.rd[key] = s
        for t in writes:
            t.wr = {key: s}
            t.rd = {}
        return ins

    def dma(self, q, fn, reads=(), writes=(), sem_tok=None, kind=None, inc=16):
        if sem_tok is None:
            sem_tok = writes[0] if kind != "r" else reads[0]
        slot = "dw" if kind != "r" else "dr"
        st = getattr(sem_tok, slot)
        if st is None:
            st = [self.newsem(sem_tok.name + slot), 0]
            setattr(sem_tok, slot, st)
        key = ("d", id(st), st[0])
        deps = self._gather(q, reads, writes)
        deps.pop(key, None)
        self._emit_waits(q, deps)
        ins = fn(self.eng[q])
        self.ninstr += 1
        st[1] += inc
        ins.then_inc(st[0], inc)
        for t in reads:
            if t.rd.get(key, 0) < st[1]:
                t.rd[key] = st[1]
        for t in writes:
            t.wr = {key: st[1]}
            t.rd = {}
        return ins

    def wait_all(self, e, toks):
        deps = {}
        for t in toks:
            for d in (t.wr, t.rd):
                for k, v in d.items():
                    if deps.get(k, 0) < v:
                        deps[k] = v
        self._emit_waits(e, deps)

    def signals(self):
        return {e: sorted(self.waited[e]) for e in self.COMPUTE}


D = 2048
SEQ = 4096
NT1 = 32
NS1 = 8
TOK2 = 1024
HALO = 128
EPS = 1e-6
LN16 = math.log(16.0)
WIN = (2, 4, 8, 16)


def build(signal=None):
    nc = bass.Bass("TRN2", target_bir_lowering=False)
    dt_in = lambda name, shape: nc.dram_tensor(name, shape, F32, kind="ExternalInput").ap()
    xb = dt_in("xb", [SEQ, D])
    xr = dt_in("xr", [HALO + TOK2, D])
    w1 = dt_in("w1", [D, 2560])
    wif = dt_in("wif", [D, 4])
    npw_t = dt_in("npw_t", [128, 16])
    cw = dt_in("cw", [128, 32])
    cb = dt_in("cb", [128, 8])
    gbias = dt_in("gbias", [2, 2])
    nwa = dt_in("nwa", [1, 512])
    ident_d = dt_in("ident", [128, 128])
    mask_d = dt_in("mask", [128, 128])
    sel_d = dt_in("sel", [2, 256])
    small = STAGE < 3
    w2 = dt_in("w2", [D, 6144] if not small else [1, 1])
    pw = dt_in("pw", [1024, 256])
    psc_t = dt_in("psc_t", [128, 8])
    invc_d = dt_in("invc", [128, 64])
    wpm = dt_in("wpm", [D, D] if not small else [1, 1])
    wpp = dt_in("wpp", [1024, D] if not small else [1, 1])
    wout = dt_in("wout", [D, D] if not small else [1, 1])
    npost = dt_in("npost", [1, D])
    out = nc.dram_tensor("out", [TOK2, D], F32, kind="ExternalOutput").ap()
    yloc = nc.dram_tensor("yloc", [4 * 512, 1024], BF16, kind="Internal").ap()
    yall = nc.dram_tensor("yall", [4 * 4 * 512, 1024], BF16, kind="Internal").ap()
    dbg = {}
    if DEBUG:
        dbg["d_ya"] = nc.dram_tensor("d_ya", [4 * 512, 1024], BF16, kind="ExternalOutput").ap()
        dbg["d_yb"] = nc.dram_tensor("d_yb", [1024, 1024], BF16, kind="ExternalOutput").ap()
        dbg["d_mg"] = nc.dram_tensor("d_mg", [2048, 1024], BF16, kind="ExternalOutput").ap()

    with contextlib.ExitStack() as st:
        P = Prog(nc, st, signal)
        ps = lambda name, dt: st.enter_context(
            nc.psum_tensor(name, [128, 512] if dt == F32 else [128, 1024], dt))
        B0 = ps("B0", BF16); tB0 = Tok("B0")
        B1 = ps("B1", BF16); tB1 = Tok("B1")
        BA = [ps(f"BA{i}", F32) for i in range(6)]
        tBA = [Tok(f"BA{i}") for i in range(6)]

        def gsb(stack, name, shape, dt):
            return stack.enter_context(nc.sbuf_tensor(name, shape, dt))

        identb = gsb(st, "identb", [128, 128], BF16); t_identb = Tok("identb")
        identf = gsb(st, "identf", [128, 128], F32); t_identf = Tok("identf")
        npw_sb = gsb(st, "npw_sb", [128, 16], F32); t_npw = Tok("npw")
        P.dma("pool", lambda e: e.dma_start(out=identb[:], in_=ident_d), writes=[t_identb])
        P.dma("sp", lambda e: e.dma_start(out=identf[:], in_=ident_d), writes=[t_identf])
        P.dma("sp", lambda e: e.dma_start(out=npw_sb[:], in_=npw_t), writes=[t_npw])

        def norm_load(xsrc_ap, xt, t_xt):
            P.dma("sp", lambda e: e.dma_start(out=xt[:], in_=xsrc_ap), writes=[t_xt])

        def norm_pre(xsrc_ap, xt, t_xt, xn, t_xn, ssq, t_ssq, load=True):
            if load:
                norm_load(xsrc_ap, xt, t_xt)
            P.op("act", lambda e: e.activation(out=xn[:], in_=xt[:], func=AF.Square, accum_out=ssq[:, 0:1]),
                 reads=[t_xt], writes=[t_xn, t_ssq])
            P.op("act", lambda e: e.activation(out=ssq[:, 1:2], in_=ssq[:, 0:1], func=AF.Sqrt,
                                               scale=1.0 / D, bias=EPS), reads=[t_ssq], writes=[t_ssq])
            P.op("dve", lambda e: e.reciprocal(out=ssq[:, 2:3], in_=ssq[:, 1:2]), reads=[t_ssq], writes=[t_ssq])
            P.op("pool", lambda e: e.tensor_scalar(out=xn[:], in0=xt[:], scalar1=ssq[:, 2:3], scalar2=1.0,
                                                   op0=ALU.mult, op1=ALU.mult),
                 reads=[t_xt, t_ssq], writes=[t_xn])

        def norm_T_r(xn, t_xn, hT_dst8, t_hT, r):
            for kk in range(8):
                k = r * 8 + kk
                P.op("pe", lambda e: e.transpose(out=B0[:, kk * 128:(kk + 1) * 128],
                                                 in_=xn[:, k * 128:(k + 1) * 128], identity=identb[:]),
                     reads=[t_xn, t_identb], writes=[tB0])
                yield
            P.op("dve", lambda e: e.tensor_tensor(
                out=hT_dst8(r), in0=B0[:, :].rearrange("p (c t) -> p c t", c=8),
                in1=npw_sb[:, r * 8:(r + 1) * 8].rearrange("p (c o) -> p c o", o=1).broadcast_to([128, 8, 128]),
                op=ALU.mult), reads=[tB0, t_npw], writes=[t_hT])
            yield

        def norm_T(xn, t_xn, hT_dst8, t_hT, evac_flip):
            for r in range(2):
                yield from norm_T_r(xn, t_xn, hT_dst8, t_hT, r)

        def interleave(main_gen, fill_gen, per_yield):
            fill = iter(fill_gen) if fill_gen is not None else iter(())
            for yi, _ in enumerate(main_gen):
                npy = per_yield[min(yi, len(per_yield) - 1)] if isinstance(per_yield, (list, tuple)) else per_yield
                for _i in range(npy):
                    if next(fill, "END") == "END":
                        break
            for _ in fill:
                pass

        def barrier(toks_extra=()):
            bt = []
            for e in ("act", "dve", "pool"):
                t = Tok("bar_" + e)
                if e == "act":
                    P.op(e, lambda h: h.copy(out=bar_tiles[e][:], in_=identf[0:2, 0:2]), reads=[t_identf], writes=[t])
                else:
                    P.op(e, lambda h: h.memset(bar_tiles[e][:], 0.0), writes=[t])
                bt.append(t)
            t = Tok("bar_pe")
            P.op("pe", lambda h: h.matmul(BA[0][0:2, 0:2], lhsT=identf[0:2, 0:2], rhs=identf[0:2, 0:2],
                                          start=True, stop=True), reads=[t_identf], writes=[t, tBA[0]])
            bt.append(t)
            for e in ("pe", "act", "dve", "pool", "sp"):
                P.wait_all(e, bt + list(toks_extra))

        bar_tiles = {e: gsb(st, "bar_" + e, [2, 2], F32) for e in ("act", "dve", "pool")}
        ln16_sb = gsb(st, "ln16_sb", [2, 1], F32); t_ln16 = Tok("ln16")
        P.op("dve", lambda e: e.memset(ln16_sb[:], LN16), writes=[t_ln16])

        t_yloc = Tok("yloc")
        ccsem = P.newsem("cc")

        with contextlib.ExitStack() as s1:
            sb = lambda name, shape, dt: gsb(s1, name, shape, dt)
            W1b = sb("W1b", [128, 16, 2560], BF16)
            tW1 = [Tok(f"W1_{i}") for i in range(5)]
            Wifb = sb("Wifb", [128, 16, 4], BF16); tWif = Tok("Wif")
            cw_sb = sb("cw_sb", [128, 32], F32); cb_sb = sb("cb_sb", [128, 8], F32); t_cw = Tok("cw")
            gb_sb = sb("gb_sb", [2, 2], F32); t_gb = Tok("gb")
            nwa_sb = sb("nwa_sb", [128, 512], F32); t_nwa = Tok("nwa")
            maskb = sb("maskb", [128, 128], BF16); t_mask = Tok("mask")
            sel_sb = sb("sel_sb", [2, 256], F32); t_sel = Tok("sel")
            xt = [sb(f"xt{i}", [128, D], F32) for i in range(2)]; t_xt = [Tok("xt0"), Tok("xt1")]
            xn = [sb(f"xn{i}", [128, D], BF16) for i in range(2)]; t_xn = [Tok("xn0"), Tok("xn1")]
            ssq = [sb(f"ssq{i}", [128, 4], F32) for i in range(2)]; t_ssq = [Tok("ssq0"), Tok("ssq1")]
            hT = [sb(f"hT{i}", [128, 16, 512], BF16) for i in range(2)]
            t_hT = [[Tok(f"hT{i}_{t}") for t in range(4)] for i in range(2)]
            qkp = [sb(f"qkp{i}", [128, 515], F32) for i in range(2)]; t_qkp = [Tok("qkp0"), Tok("qkp1")]
            halo3 = sb("halo3", [128, 8, 3], F32); t_halo = [Tok(f"halo{c}") for c in range(8)]
            cacc = [sb(f"cacc{i}", [128, 512], F32) for i in range(2)]; t_cacc = [Tok("cacc0"), Tok("cacc1")]
            qkT = sb("qkT", [128, 8, 512], BF16); t_qkT = [Tok(f"qkT{c}") for c in range(8)]
            zeros2 = sb("zeros2", [2, 512], F32); t_zeros = Tok("zeros2")
            g_li = sb("g_li", [2, 512], F32); g_f = sb("g_f", [2, 512], F32)
            g_nB = sb("g_nB", [2, 512], F32); g_a = sb("g_a", [2, 512], F32)
            g_M = sb("g_M", [2, 512], F32); g_t = sb("g_t", [2, 512], F32)
            g_u = sb("g_u", [2, 512], F32); g_e = sb("g_e", [2, 512], F32)
            g_sm = sb("g_sm", [2, 16], F32)
            t_g = Tok("gates")
            ucol = sb("ucol", [128, 4, 2, 2], F32); t_ucol = Tok("ucol")
            decb = sb("decb", [128, 2, 4], F32); t_decb = Tok("decb")
            vs2 = [sb(f"vs{i}", [128, 2, 257], BF16) for i in range(2)]; t_vs2 = [Tok("vs0"), Tok("vs1")]
            so2_ = [sb(f"so{i}", [128, 512], F32) for i in range(2)]; t_so2_ = [Tok("so0"), Tok("so1")]
            sz2_ = [sb(f"sz{i}", [128, 512], F32) for i in range(2)]; t_sz2_ = [Tok("sz0"), Tok("sz1")]
            gg2 = [sb(f"gg{i}", [128, 512], F32) for i in range(2)]; t_gg2 = [Tok("gg0"), Tok("gg1")]
            k_tm = sb("k_tm", [128, 512], BF16); t_ktm = Tok("ktm")
            mS2 = [sb(f"mS{i}", [128, 128], BF16) for i in range(2)]; t_mS2 = [Tok("mS0"), Tok("mS1")]
            Cst = sb("Cst", [128, 2, 2, 257], F32); t_C = [Tok("C0"), Tok("C1")]
            Cd = sb("Cd", [128, 2, 2, 257], BF16); t_Cd = [Tok("Cd0"), Tok("Cd1")]
            ep = sb("ep", [128, 2, 8], F32); t_ep = [Tok("ep0"), Tok("ep1")]
            ES = sb("ES", [128, 2], F32); t_es = [Tok("es0"), Tok("es1")]
            junk = sb("junk", [128, 256], BF16); t_junk = Tok("junk")
            ya2 = [sb(f"ya{i}", [128, 512], BF16) for i in range(2)]; t_ya2 = [Tok("ya0"), Tok("ya1")]
            yaT = [sb(f"yaT{i}", [128, 4, 128], BF16) for i in range(2)]; t_yaT = [Tok("yaT0"), Tok("yaT1")]

            for blk in range(5):
                for hf in range(2):
                    P.dma("pool", lambda e: e.dma_start(
                        out=W1b[:, hf * 8:(hf + 1) * 8, blk * 512:(blk + 1) * 512],
                        in_=w1[hf * 1024:(hf + 1) * 1024, blk * 512:(blk + 1) * 512].rearrange("(c p) n -> p c n", p=128)),
                        writes=[tW1[blk]])
                if blk == 0:
                    P.dma("pool", lambda e: e.dma_start(out=Wifb[:], in_=wif.rearrange("(c p) n -> p c n", p=128)),
                          writes=[tWif])
                    P.dma("pool", lambda e: e.dma_start(out=maskb[:], in_=mask_d), writes=[t_mask])
            P.dma("sp", lambda e: e.dma_start(out=cw_sb[:], in_=cw), writes=[t_cw])
            P.dma("sp", lambda e: e.dma_start(out=cb_sb[:], in_=cb), writes=[t_cw])
            P.dma("sp", lambda e: e.dma_start(out=gb_sb[:], in_=gbias), writes=[t_gb])
            P.dma("sp", lambda e: e.dma_start(out=nwa_sb[:], in_=nwa.partition_broadcast(128)), writes=[t_nwa])
            P.dma("sp", lambda e: e.dma_start(out=sel_sb[:], in_=sel_d), writes=[t_sel])
            P.op("dve", lambda e: e.memset(zeros2[:], 0.0), writes=[t_zeros])
            P.op("dve", lambda e: e.memset(g_sm[:], 0.0), writes=[t_g])
            P.op("pool", lambda e: e.memset(halo3[:], 0.0), writes=t_halo)

            def n1_args(tile):
                s_, tt = divmod(tile, 4)
                return s_ % 2, tt, tile % 2

            def emit_norm1_pre(tile, load=True):
                hb, tt, xi = n1_args(tile)
                norm_pre(xb[tile * 128:(tile + 1) * 128, :], xt[xi], t_xt[xi], xn[xi], t_xn[xi], ssq[xi], t_ssq[xi], load)

            def emit_norm1_load(tile):
                hb, tt, xi = n1_args(tile)
                norm_load(xb[tile * 128:(tile + 1) * 128, :], xt[xi], t_xt[xi])

            def gen_norm1_T(tile, r=None):
                hb, tt, xi = n1_args(tile)
                dst = lambda r_: hT[hb][:, r_ * 8:(r_ + 1) * 8, tt * 128:(tt + 1) * 128]
                if r is None:
                    return norm_T(xn[xi], t_xn[xi], dst, t_hT[hb][tt], tile % 2)
                return norm_T_r(xn[xi], t_xn[xi], dst, t_hT[hb][tt], r)

            for tile in range(4):
                emit_norm1_pre(tile)
                for _ in gen_norm1_T(tile):
                    pass

            BS, tBS = BA[2], tBA[2]
            BN, tBN = BA[3], tBA[3]
            BK = [BA[4], BA[5]]; tBK = [tBA[4], tBA[5]]
            acc_i = [0]

            def next_acc(wide=False):
                pool_ = (0, 1, 3, 4, 5) if wide else (0, 1)
                i = pool_[acc_i[0] % len(pool_)]
                acc_i[0] += 1
                return BA[i], tBA[i]

            def gen_proj(s_, tt, wide=False):
                hTs = hT[s_ % 2]; t_hTs = t_hT[s_ % 2]
                pb = tt % 2
                vs = vs2[pb]; t_vs = t_vs2[pb]
                tsl = slice(tt * 128, (tt + 1) * 128)
                for blk, nm in ((2, "v"), (3, "o"), (4, "z")):
                    P_acc, t_acc = next_acc(wide)
                    for k in range(16):
                        P.op("pe", lambda e: e.matmul(P_acc[:, :], lhsT=hTs[:, k, tsl], rhs=W1b[:, k, blk * 512:(blk + 1) * 512],
                                                      start=(k == 0), stop=(k == 15)),
                             reads=[t_hTs[tt], tW1[blk]], writes=[t_acc])
                        yield
                    if nm == "v":
                        for h in range(2):
                            P.op("dve", lambda e: e.tensor_scalar(out=vs[:, h, 0:256], in0=P_acc[:, h * 256:(h + 1) * 256],
                                                                  scalar1=ucol[:, tt, 0, h:h + 1], scalar2=None, op0=ALU.mult),
                                 reads=[t_acc, t_ucol], writes=[t_vs])
                        P.op("dve", lambda e: e.tensor_copy(out=vs[:, :, 256], in_=ucol[:, tt, 0, :]),
                             reads=[t_ucol], writes=[t_vs])
                    elif nm == "o":
                        P.op("act", lambda e: e.activation(out=so2_[pb][:], in_=P_acc[:, :], func=AF.Sigmoid),
                             reads=[t_acc], writes=[t_so2_[pb]])
                    else:
                        P.op("act", lambda e: e.activation(out=sz2_[pb][:], in_=P_acc[:, :], func=AF.Silu),
                             reads=[t_acc], writes=[t_sz2_[pb]])
                P.op("pool", lambda e: e.tensor_tensor(out=gg2[pb][:], in0=so2_[pb][:], in1=sz2_[pb][:], op=ALU.mult),
                     reads=[t_so2_[pb], t_sz2_[pb]], writes=[t_gg2[pb]])
                P.op("pool", lambda e: e.tensor_tensor(out=gg2[pb][:], in0=gg2[pb][:], in1=nwa_sb[:], op=ALU.mult),
                     reads=[t_gg2[pb], t_nwa], writes=[t_gg2[pb]])
                yield

            def finish_tile(tile):
                yb_i = tile % 2
                ya = ya2[yb_i]; t_ya = t_ya2[yb_i]
                for j in range(4):
                    P.op("pe", lambda e: e.transpose(out=B1[:, 512 + j * 128:512 + (j + 1) * 128], in_=ya[:, j * 128:(j + 1) * 128],
                                                     identity=identb[:]), reads=[t_ya, t_identb], writes=[tB1])
                P.op("act", lambda e: e.copy(out=yaT[yb_i][:].rearrange("p j t -> p (j t)"), in_=B1[:, 512:1024]),
                     reads=[tB1], writes=[t_yaT[yb_i]])
                tq, to = divmod(tile, 8)
                P.dma("sp", lambda e: e.dma_start(
                    out=yloc[tq * 512:(tq + 1) * 512, to * 128:(to + 1) * 128].rearrange("(j p) n -> p j n", p=128),
                    in_=yaT[yb_i][:]), reads=[t_yaT[yb_i]], writes=[t_yloc], sem_tok=t_yaT[yb_i], kind="r")
                if tile % 8 == 7:
                    P.wait_all("pool", [t_yloc])
                    nc.gpsimd.collective_compute(
                        "AllGather", ALU.bypass, replica_groups=[[0, 1, 2, 3], [4, 5, 6, 7]],
                        ins=[yloc[tq * 512:(tq + 1) * 512, :]],
                        outs=[yall[tq * 2048:(tq + 1) * 2048, :]]).then_inc(ccsem, 1)

            def gen_mlstm(s_, tt, pre_hook=None):
                tile = s_ * 4 + tt
                pb = tt % 2
                vs = vs2[pb]; t_vs = t_vs2[pb]
                gg = gg2[pb]; t_gg = t_gg2[pb]
                ya = ya2[tile % 2]; t_ya = t_ya2[tile % 2]
                tsl = slice(tt * 128, (tt + 1) * 128)
                if tile > 0:
                    for h in range(2):
                        Ch = Cst[:, h].rearrange("p a b -> p (a b)")
                        Cdh = Cd[:, h].rearrange("p a b -> p (a b)")
                        P.op("pool", lambda e: e.tensor_scalar(out=Cdh, in0=Ch, scalar1=decb[:, h, tt:tt + 1], scalar2=1.0,
                                                               op0=ALU.mult, op1=ALU.mult),
                             reads=[t_C[h], t_decb], writes=[t_Cd[h]])
                if pre_hook is not None:
                    pre_hook(0)
                for h in range(2):
                    for dkc in range(2):
                        cc = 4 + 2 * h + dkc
                        P.op("pe", lambda e: e.transpose(out=B1[:, (2 * h + dkc) * 128:(2 * h + dkc + 1) * 128],
                                                         in_=qkT[:, cc, tsl], identity=identb[:]),
                             reads=[t_qkT[cc], t_identb], writes=[tB1])
                P.op("act", lambda e: e.copy(out=k_tm[:], in_=B1[:, 0:512]), reads=[tB1], writes=[t_ktm])
                yield
                for h in range(2):
                    for dkc in range(2):
                        P.op("pe", lambda e: e.matmul(BS[:, h * 256:h * 256 + 128], lhsT=qkT[:, 4 + 2 * h + dkc, tsl],
                                                      rhs=qkT[:, 2 * h + dkc, tsl], start=(dkc == 0), stop=(dkc == 1)),
                             reads=[t_qkT[4 + 2 * h + dkc], t_qkT[2 * h + dkc]], writes=[tBS])
                for h in range(2):
                    P.op("dve", lambda e: e.tensor_tensor(out=mS2[h][:], in0=BS[:, h * 256:h * 256 + 128], in1=maskb[:], op=ALU.mult),
                         reads=[tBS, t_mask], writes=[t_mS2[h]])
                yield
                for h in range(2):
                    mS = mS2[h]; t_mS = t_mS2[h]
                    P.op("pe", lambda e: e.matmul(BN[:, 0:257], lhsT=mS[:], rhs=vs[:, h, :], start=True, stop=(tile == 0)),
                         reads=[t_mS, t_vs], writes=[tBN])
                    if tile > 0:
                        for dkc in range(2):
                            P.op("pe", lambda e: e.matmul(BN[:, 0:257], lhsT=qkT[:, 2 * h + dkc, tsl], rhs=Cd[:, h, dkc, :],
                                                          start=False, stop=(dkc == 1)),
                                 reads=[t_qkT[2 * h + dkc], t_Cd[h]], writes=[tBN])
                    for dkc in range(2):
                        P.op("pe", lambda e: e.matmul(BK[dkc][:, 0:257], lhsT=k_tm[:, (2 * h + dkc) * 128:(2 * h + dkc + 1) * 128],
                                                      rhs=vs[:, h, :], start=True, stop=True),
                             reads=[t_ktm, t_vs], writes=[tBK[dkc]])
                        if tile == 0:
                            P.op("dve", lambda e: e.tensor_copy(out=Cst[:, h, dkc, :], in_=BK[dkc][:, 0:257]),
                                 reads=[tBK[dkc]], writes=[t_C[h]])
                        else:
                            P.op("dve", lambda e: e.scalar_tensor_tensor(out=Cst[:, h, dkc, :], in0=Cst[:, h, dkc, :],
                                                                         scalar=decb[:, h, tt:tt + 1], in1=BK[dkc][:, 0:257],
                                                                         op0=ALU.mult, op1=ALU.add),
                                 reads=[tBK[dkc], t_decb, t_Cd[h]], writes=[t_C[h]])
                    E = ep[:, h]
                    te = t_ep[h]
                    P.op("act", lambda e: e.activation(out=junk[:], in_=BN[:, 0:256], func=AF.Square, scale=1.0 / 16,
                                                       accum_out=ES[:, h:h + 1]),
                         reads=[tBN], writes=[t_junk, t_es[h]])
                    P.op("dve", lambda e: e.tensor_tensor(out=E[:, 0:1], in0=BN[:, 256:257], in1=ucol[:, tt, 1, h:h + 1],
                                                          op=ALU.max), reads=[tBN, t_ucol], writes=[te])
                    P.op("dve", lambda e: e.scalar_tensor_tensor(out=E[:, 1:2], in0=BN[:, 256:257], scalar=-1.0, in1=E[:, 0:1],
                                                                 op0=ALU.mult, op1=ALU.max), reads=[tBN, te], writes=[te])
                    P.op("dve", lambda e: e.scalar_tensor_tensor(out=E[:, 3:4], in0=E[:, 1:2], scalar=EPS, in1=E[:, 1:2],
                                                                 op0=ALU.mult, op1=ALU.mult), reads=[te], writes=[te])
                    P.op("act", lambda e: e.activation(out=E[:, 4:5], in_=E[:, 3:4], func=AF.Sqrt, bias=ES[:, h:h + 1], scale=1.0),
                         reads=[te, t_es[h]], writes=[te])
                    P.op("dve", lambda e: e.reciprocal(out=E[:, 6:7], in_=E[:, 4:5]), reads=[te], writes=[te])
                    P.op("dve", lambda e: e.scalar_tensor_tensor(out=ya[:, h * 256:(h + 1) * 256], in0=BN[:, 0:256],
                                                                 scalar=E[:, 6:7], in1=gg[:, h * 256:(h + 1) * 256],
                                                                 op0=ALU.mult, op1=ALU.mult),
                         reads=[tBN, te, t_gg], writes=[t_ya])
                    if h == 0 and tile > 0:
                        finish_tile(tile - 1)
                    if h == 0 and pre_hook is not None:
                        pre_hook(1)
                    yield

            def chain(*gens):
                for g in gens:
                    if g is not None:
                        yield from g

            for s_ in range(NS_RUN):
                hb = s_ % 2
                hTs = hT[hb]
                t_hTs = t_hT[hb]
                for gi in range(2):
                    P_acc, t_acc = BS, tBS
                    for k in range(16):
                        P.op("pe", lambda e: e.matmul(P_acc[0:2, 0:512],
                                                      lhsT=Wifb[:, k, 2 * gi:2 * gi + 2], rhs=hTs[:, k, :],
                                                      start=(k == 0), stop=(k == 15)),
                             reads=t_hTs + [tWif], writes=[t_acc])
                    dst = g_li if gi == 0 else g_f
                    P.op("act", lambda e: e.activation(out=dst[:], in_=P_acc[0:2, 0:512], func=AF.Identity,
                                                       bias=gb_sb[:, gi:gi + 1]),
                         reads=[t_acc, t_gb], writes=[t_g])
                G = lambda eng, fn: P.op(eng, fn, reads=[t_g, t_zeros], writes=[t_g])
                G("act", lambda e: e.activation(out=g_f[:], in_=g_f[:], func=AF.Exp, scale=-1.0))
                G("act", lambda e: e.activation(out=g_f[:], in_=g_f[:], func=AF.Ln, bias=1.0))
                G("dve", lambda e: e.tensor_tensor_scan(out=g_nB[:], data0=g_f[:], data1=zeros2[:],
                                                        initial=g_sm[:, 0:1], op0=ALU.add, op1=ALU.add))
                G("dve", lambda e: e.tensor_tensor(out=g_a[:], in0=g_li[:], in1=g_nB[:], op=ALU.add))
                G("dve", lambda e: e.tensor_tensor_scan(out=g_M[:], data0=g_a[:], data1=zeros2[:],
                                                        initial=g_sm[:, 1:2], op0=ALU.max, op1=ALU.add))
                M3 = g_M[:].rearrange("p (c t) -> p c t", t=128)
                Mend = M3[:, :, 127:128]
                G("dve", lambda e: e.tensor_copy(out=g_sm[:, 4:5], in_=g_sm[:, 1:2]))
                G("dve", lambda e: e.tensor_copy(out=g_sm[:, 5:8], in_=g_M[:, 127:127 + 3 * 128:128]))
                G("dve", lambda e: e.tensor_tensor(out=g_sm[:, 8:12], in0=g_sm[:, 4:8], in1=g_M[:, 127::128],
                                                   op=ALU.subtract))
                G("act", lambda e: e.activation(out=g_sm[:, 12:16], in_=g_sm[:, 8:12], func=AF.Exp))
                G("dve", lambda e: e.tensor_copy(out=g_sm[:, 0:1], in_=g_nB[:, 511:512]))
                G("dve", lambda e: e.tensor_copy(out=g_sm[:, 1:2], in_=g_M[:, 511:512]))
                G("dve", lambda e: e.tensor_tensor(out=g_t[:].rearrange("p (c t) -> p c t", t=128),
                                                   in0=g_a[:].rearrange("p (c t) -> p c t", t=128),
                                                   in1=Mend.broadcast_to([2, 4, 128]), op=ALU.subtract))
                G("act", lambda e: e.activation(out=g_u[:], in_=g_t[:], func=AF.Exp))
                G("dve", lambda e: e.tensor_tensor(out=g_a[:].rearrange("p (c t) -> p c t", t=128),
                                                   in0=g_nB[:].rearrange("p (c t) -> p c t", t=128),
                                                   in1=Mend.broadcast_to([2, 4, 128]), op=ALU.subtract))
                G("act", lambda e: e.activation(out=g_e[:], in_=g_a[:], func=AF.Exp, bias=ln16_sb[:, 0:1]))
                for cc in range(8):
                    P_acc, t_acc = next_acc(wide=True)
                    for k in range(16):
                        P.op("pe", lambda e: e.matmul(P_acc[:, :], lhsT=W1b[:, k, cc * 128:(cc + 1) * 128], rhs=hTs[:, k, :],
                                                      start=(k == 0), stop=(k == 15)),
                             reads=t_hTs + [tW1[cc // 4]], writes=[t_acc])
                    qp = qkp[cc % 2]; t_qp = t_qkp[cc % 2]
                    P.op("pool", lambda e: e.tensor_copy(out=qp[:, 0:3], in_=halo3[:, cc, :]),
                         reads=[t_halo[cc]], writes=[t_qp])
                    P.op("act", lambda e: e.copy(out=qp[:, 3:515], in_=P_acc[:, :]),
                         reads=[t_acc], writes=[t_qp])
                    P.op("pool", lambda e: e.tensor_copy(out=halo3[:, cc, :], in_=qp[:, 512:515]),
                         reads=[t_qp], writes=[t_halo[cc]])
                    ca = cacc[cc % 2]; t_ca = t_cacc[cc % 2]
                    P.op("dve", lambda e: e.tensor_scalar(out=ca[:], in0=qp[:, 0:512],
                                                          scalar1=cw_sb[:, cc * 4:cc * 4 + 1], scalar2=cb_sb[:, cc:cc + 1],
                                                          op0=ALU.mult, op1=ALU.add),
                         reads=[t_qp, t_cw], writes=[t_ca])
                    for j in range(1, 4):
                        P.op("dve", lambda e: e.scalar_tensor_tensor(out=ca[:], in0=qp[:, j:j + 512],
                                                                     scalar=cw_sb[:, cc * 4 + j:cc * 4 + j + 1], in1=ca[:],
                                                                     op0=ALU.mult, op1=ALU.add),
                             reads=[t_qp, t_cw], writes=[t_ca])
                    P.op("act", lambda e: e.activation(out=qkT[:, cc, :], in_=ca[:], func=AF.Silu),
                         reads=[t_ca], writes=[t_qkT[cc]])
                for c in range(4):
                    for qi, src in enumerate((g_u, g_e)):
                        col = 128 + (c * 2 + qi) * 2
                        P.op("pe", lambda e: e.transpose(out=BS[:, col:col + 2], in_=src[:, c * 128:(c + 1) * 128],
                                                         identity=identf[0:2, 0:2]),
                             reads=[t_g, t_identf], writes=[tBS])
                for h in range(2):
                    P.op("pe", lambda e: e.matmul(BS[:, 160 + 4 * h:164 + 4 * h], lhsT=sel_sb[:, h * 128:(h + 1) * 128],
                                                  rhs=g_sm[:, 12:16], start=True, stop=True),
                         reads=[t_g, t_sel], writes=[tBS])
                P.op("dve", lambda e: e.tensor_copy(out=ucol[:].rearrange("p c q h -> p (c q h)"), in_=BS[:, 128:144]),
                     reads=[tBS], writes=[t_ucol])
                P.op("dve", lambda e: e.tensor_copy(out=decb[:].rearrange("p h c -> p (h c)"), in_=BS[:, 160:168]),
                     reads=[tBS], writes=[t_decb])
                for _ in gen_proj(s_, 0, wide=True):
                    pass
                for tt in range(4):
                    tile = s_ * 4 + tt
                    nxt = tile + 4 < NS_RUN * 4
                    if tile == 0 and nxt:
                        emit_norm1_pre(4)
                    hook = ((lambda st, t5=tile + 5: emit_norm1_load(t5) if st == 0 else emit_norm1_pre(t5, load=False))
                            if tile + 5 < NS_RUN * 4 else None)
                    pj = gen_proj(s_, tt + 1) if tt < 3 else iter(())
                    fill = chain(itertools.islice(pj, 24),
                                 gen_norm1_T(tile + 4, 0) if nxt else None,
                                 pj,
                                 gen_norm1_T(tile + 4, 1) if nxt else None)
                    interleave(gen_mlstm(s_, tt, hook), fill, [10, 8, 30, 100])
            finish_tile(NS_RUN * 4 - 1)
            barrier([t_yloc])
        if STAGE == 1:
            t_dd = Tok("dd")
            P.dma("sp", lambda e: e.dma_start(out=out[0:128, 0:128], in_=identf[:]), reads=[t_identf], kind="r", sem_tok=t_dd)
            if DEBUG:
                P.dma("sp", lambda e: e.dma_start(out=dbg["d_ya"], in_=yloc), reads=[t_yloc], kind="r", sem_tok=t_dd)
            P.wait_all("sp", [t_identf, t_yloc])
            return nc, P.signals(), P.ninstr
        if STAGE == 2:
            with contextlib.ExitStack() as s2x:
                yaTr = gsb(s2x, "yaTr", [128, 16, TOK2], BF16); t_yaTr = Tok("yaTr")
                nc.sync.wait_ge(ccsem, 4)
                pid = nc.sync.partition_id()
                tqv = pid % 4
                for r in range(4):
                    P.dma("sp", lambda e: e.dma_start(
                        out=yaTr[:, r * 4:(r + 1) * 4, :],
                        in_=yall[bass.ts(tqv * 4 + r, 512), :].rearrange("(j p) n -> p j n", p=128)),
                        writes=[t_yaTr])
                P.dma("sp", lambda e: e.dma_start(out=dbg["d_ya"].rearrange("(c p) n -> p c n", p=128), in_=yaTr[:]),
                      reads=[t_yaTr], kind="r", sem_tok=t_yaTr)
                P.wait_all("sp", [t_yaTr])
                nc.gpsimd.wait_ge(ccsem, 4)
            return nc, P.signals(), P.ninstr
        with contextlib.ExitStack() as s2:
            sb = lambda name, shape, dt: gsb(s2, name, shape, dt)
            hT2 = sb("hT2", [128, 16, HALO + TOK2], BF16); t_hT2 = [Tok(f"hT2_{i}") for i in range(9)]
            ybT = sb("ybT", [128, 8, TOK2], BF16); t_ybT = [Tok(f"ybT{i}") for i in range(8)]
            psc_sb = sb("psc_sb", [128, 8], F32); t_psc = Tok("psc")
            P.dma("sp", lambda e: e.dma_start(out=psc_sb[:], in_=psc_t), writes=[t_psc])
            acc2 = [0]

            def next_acc2():
                i = acc2[0] % 6
                acc2[0] += 1
                return BA[i], tBA[i]

            with contextlib.ExitStack() as s2a:
                sa = lambda name, shape, dt: gsb(s2a, name, shape, dt)
                xt2 = [sa(f"x2t{i}", [128, D], F32) for i in range(2)]; t_xt2 = [Tok("x2t0"), Tok("x2t1")]
                xn2 = [sa(f"x2n{i}", [128, D], BF16) for i in range(2)]; t_xn2 = [Tok("x2n0"), Tok("x2n1")]
                ssq2 = [sa(f"ssq2{i}", [128, 4], F32) for i in range(2)]; t_ssq2 = [Tok("ssq20"), Tok("ssq21")]
                def pre2(tile):
                    i = tile % 2
                    norm_pre(xr[tile * 128:(tile + 1) * 128, :], xt2[i], t_xt2[i], xn2[i], t_xn2[i], ssq2[i], t_ssq2[i])
                pre2(0)
                for tile in range(9):
                    i = tile % 2
                    if tile + 1 < 9:
                        pre2(tile + 1)
                    for _ in norm_T(xn2[i], t_xn2[i], lambda r: hT2[:, r * 8:(r + 1) * 8, tile * 128:(tile + 1) * 128],
                                    t_hT2[tile], tile % 2):
                        pass
                Wb = [sa(f"Wb{i}", [128, 16, 512], BF16) for i in range(2)]; t_Wb = [Tok("Wb0"), Tok("Wb1")]
                uT = sa("uT", [128, 8, HALO + TOK2], F32); t_uT = [Tok(f"uT{i}") for i in range(8)]
                szb = sa("szb", [128, 8, TOK2], BF16); t_szb = [Tok(f"szb{i}") for i in range(8)]
                plT = sa("plT", [128, 8, TOK2], BF16); t_plT = [Tok(f"plT{i}") for i in range(8)]
                sA = sa("sA", [128, HALO + TOK2], F32); sB = sa("sB", [128, HALO + TOK2], F32)
                t_sA = Tok("sA"); t_sB = Tok("sB")
                pwb = sa("pwb", [128, 8, 256], BF16); t_pwb = Tok("pwb")
                invc = sa("invc_sb", [128, 64], F32); t_invc = Tok("invc")
                tmp16 = sa("tmp16", [128, 16], F32); t_tmp16 = Tok("tmp16")
                P.dma("pool", lambda e: e.dma_start(out=pwb[:], in_=pw.rearrange("(c p) n -> p c n", p=128)), writes=[t_pwb])
                P.dma("sp", lambda e: e.dma_start(out=invc[:], in_=invc_d), writes=[t_invc])
                def load_wb(wb):
                    W = Wb[wb % 2]; tW = t_Wb[wb % 2]
                    for hf in range(2):
                        P.dma("pool", lambda e: e.dma_start(
                            out=W[:, hf * 8:(hf + 1) * 8, :],
                            in_=w2[hf * 1024:(hf + 1) * 1024, wb * 512:(wb + 1) * 512].rearrange("(c p) n -> p c n", p=128)),
                            writes=[tW])
                load_wb(0)
                for wb in range(4):
                    W = Wb[wb % 2]; tW = t_Wb[wb % 2]
                    if wb + 1 < 4:
                        load_wb(wb + 1)
                    for ci in range(4):
                        cc = wb * 4 + ci
                        for n in range(3 if cc < 8 else 2):
                            lo, hi = (HALO + n * 512, HALO + (n + 1) * 512) if n < 2 else (0, HALO)
                            P_acc, t_acc = next_acc2()
                            for k in range(16):
                                P.op("pe", lambda e: e.matmul(P_acc[:, 0:hi - lo], lhsT=W[:, k, ci * 128:(ci + 1) * 128],
                                                              rhs=hT2[:, k, lo:hi], start=(k == 0), stop=(k == 15)),
                                     reads=t_hT2 + [tW], writes=[t_acc])
                            if cc < 8:
                                P.op("act", lambda e: e.copy(out=uT[:, cc, lo:hi], in_=P_acc[:, 0:hi - lo]),
                                     reads=[t_acc], writes=[t_uT[cc]])
                            else:
                                P.op("act", lambda e: e.activation(out=szb[:, cc - 8, lo - HALO:hi - HALO], in_=P_acc[:, :],
                                                                   func=AF.Silu), reads=[t_acc], writes=[t_szb[cc - 8]])
                        if cc < 8:
                            g = cc // 2
                            u_ = uT[:, cc, :]
                            cur, t_cur = u_, t_uT[cc]
                            bufs = [(sA, t_sA), (sB, t_sB)]
                            sh = 1
                            NB = HALO + TOK2
                            for step in range(g + 1):
                                dst, t_dst = bufs[step % 2]
                                P.op("dve", lambda e: e.tensor_tensor(out=dst[:, 64:NB], in0=cur[:, 64:NB],
                                                                      in1=cur[:, 64 - sh:NB - sh], op=ALU.add),
                                     reads=[t_cur], writes=[t_dst])
                                cur, t_cur = dst, t_dst
                                sh *= 2
                            src, t_src = cur, t_cur
                            w_ = float(WIN[g])
                            P.op("dve", lambda e: e.scalar_tensor_tensor(out=plT[:, cc, :], in0=src[:, HALO:], scalar=1.0 / w_,
                                                                         in1=u_[:, HALO:], op0=ALU.mult, op1=ALU.subtract),
                                 reads=[t_src, t_uT[cc]], writes=[t_plT[cc]])
                            P.op("dve", lambda e: e.tensor_tensor(out=tmp16[:], in0=src[:, HALO:HALO + 16],
                                                                  in1=invc[:, g * 16:(g + 1) * 16], op=ALU.mult),
                                 reads=[t_src, t_invc], writes=[t_tmp16])
                            P.op("dve", lambda e: e.tensor_tensor(out=plT[:, cc, 0:16], in0=tmp16[:], in1=u_[:, HALO:HALO + 16],
                                                                  op=ALU.subtract),
                                 reads=[t_tmp16, t_uT[cc]], writes=[t_plT[cc]])
                for g in range(4):
                    for dc in range(2):
                        ch = g * 2 + dc
                        for n in range(2):
                            P_acc, t_acc = next_acc2()
                            for kc in range(2):
                                P.op("pe", lambda e: e.matmul(P_acc[:, :], lhsT=pwb[:, g * 2 + kc, dc * 128:(dc + 1) * 128],
                                                              rhs=plT[:, g * 2 + kc, n * 512:(n + 1) * 512],
                                                              start=(kc == 0), stop=(kc == 1)),
                                     reads=[t_pwb, t_plT[g * 2 + kc]], writes=[t_acc])
                            P.op("dve", lambda e: e.scalar_tensor_tensor(out=ybT[:, ch, n * 512:(n + 1) * 512], in0=P_acc[:, :],
                                                                         scalar=psc_sb[:, ch:ch + 1],
                                                                         in1=szb[:, ch, n * 512:(n + 1) * 512],
                                                                         op0=ALU.mult, op1=ALU.mult),
                                 reads=[t_acc, t_psc, t_szb[ch]], writes=[t_ybT[ch]])
                if DEBUG:
                    P.dma("sp", lambda e: e.dma_start(out=dbg["d_yb"].rearrange("(c p) n -> p c n", p=128), in_=ybT[:]),
                          reads=t_ybT, kind="r", sem_tok=t_ybT[0])
                barrier(t_ybT)
            mgT = sb("mgT", [128, 16, TOK2], BF16); t_mgT = [Tok(f"mgT{i}") for i in range(16)]
            with contextlib.ExitStack() as s2b:
                sa = lambda name, shape, dt: gsb(s2b, name, shape, dt)
                yaTr = sa("yaTr", [128, 16, TOK2], BF16); t_yaTr = Tok("yaTr")
                WG = [[sa(f"WG{i}_{j}", [128, 16 if j < 3 else 8, 256], BF16) for j in range(4)] for i in range(2)]
                t_WG = [[Tok(f"WG{i}_{j}") for j in range(4)] for i in range(2)]
                sg = [sa(f"sg{i}", [128, 512], F32) for i in range(2)]; t_sg = [Tok("sg0"), Tok("sg1")]
                m1 = [sa(f"m1{i}", [128, 512], F32) for i in range(2)]; t_m1 = [Tok("m10"), Tok("m11")]
                nc.sync.wait_ge(ccsem, 4)
                pid = nc.sync.partition_id()
                tqv = pid % 4
                for r in range(4):
                    P.dma("sp", lambda e: e.dma_start(
                        out=yaTr[:, r * 4:(r + 1) * 4, :],
                        in_=yall[bass.ts(tqv * 4 + r, 512), :].rearrange("(j p) n -> p j n", p=128)),
                        writes=[t_yaTr])
                if DEBUG:
                    P.dma("sp", lambda e: e.dma_start(out=dbg["d_ya"].rearrange("(c p) n -> p c n", p=128), in_=yaTr[:]),
                          reads=[t_yaTr], kind="r", sem_tok=t_yaTr)
                srcs = [(w2, 2048, 16), (w2, 4096, 16), (wpm, 0, 16), (wpp, 0, 8)]
                def load_wg(dg):
                    bi = dg % 2
                    for j, (wsrc, coff, nk) in enumerate(srcs):
                        P.dma("pool", lambda e: e.dma_start(
                            out=WG[bi][j][:],
                            in_=wsrc[0:nk * 128, coff + dg * 256:coff + (dg + 1) * 256].rearrange("(c p) n -> p c n", p=128)),
                            writes=[t_WG[bi][j]])
                load_wg(0)
                for dg in range(8):
                    bi = dg % 2
                    if dg + 1 < 8:
                        load_wg(dg + 1)
                    for di in range(2):
                        dmc = dg * 2 + di
                        dsl = slice(di * 128, (di + 1) * 128)
                        for n in range(2):
                            nsl = slice(n * 512, (n + 1) * 512)
                            hsl = slice(HALO + n * 512, HALO + (n + 1) * 512)
                            for br in range(2):
                                P_g, t_g2 = next_acc2()
                                for k in range(16):
                                    P.op("pe", lambda e: e.matmul(P_g[:, :], lhsT=WG[bi][br][:, k, dsl], rhs=hT2[:, k, hsl],
                                                                  start=(k == 0), stop=(k == 15)),
                                         reads=t_hT2 + [t_WG[bi][br]], writes=[t_g2])
                                P.op("act", lambda e: e.activation(out=sg[br][:], in_=P_g[:, :], func=AF.Sigmoid),
                                     reads=[t_g2], writes=[t_sg[br]])
                                P_b, t_b = next_acc2()
                                nk = 16 if br == 0 else 8
                                actT = yaTr if br == 0 else ybT
                                rds = [t_yaTr] if br == 0 else t_ybT
                                for k in range(nk):
                                    P.op("pe", lambda e: e.matmul(P_b[:, :], lhsT=WG[bi][2 + br][:, k, dsl], rhs=actT[:, k, nsl],
                                                                  start=(k == 0), stop=(k == nk - 1)),
                                         reads=rds + [t_WG[bi][2 + br]], writes=[t_b])
                                P.op("dve", lambda e: e.tensor_tensor(out=m1[br][:], in0=P_b[:, :], in1=sg[br][:], op=ALU.mult),
                                     reads=[t_b, t_sg[br]], writes=[t_m1[br]])
                            P.op("dve", lambda e: e.tensor_tensor(out=mgT[:, dmc, nsl], in0=m1[0][:], in1=m1[1][:], op=ALU.add),
                                 reads=t_m1, writes=[t_mgT[dmc]])
                if DEBUG:
                    P.dma("sp", lambda e: e.dma_start(out=dbg["d_mg"].rearrange("(c p) n -> p c n", p=128), in_=mgT[:]),
                          reads=t_mgT, kind="r", sem_tok=t_mgT[0])
                barrier(t_mgT)
            with contextlib.ExitStack() as s2c:
                sa = lambda name, shape, dt: gsb(s2c, name, shape, dt)
                Wo = sa("Wo", [128, 16, D], BF16); t_Wo = [Tok(f"Wo{i}") for i in range(4)]
                npo = sa("npo", [128, D], F32); t_npo = Tok("npo")
                ob = [sa(f"ob{i}", [128, D], F32) for i in range(2)]; t_ob = [Tok("ob0"), Tok("ob1")]
                jk = sa("jk", [128, D], BF16); t_jk = Tok("jk")
                xres = [sa(f"xres{i}", [128, D], F32) for i in range(2)]; t_xres = [Tok("xres0"), Tok("xres1")]
                so2 = [sa(f"so2{i}", [128, 4], F32) for i in range(2)]; t_so2 = [Tok("so20"), Tok("so21")]
                for cbk in range(4):
                    for hf in range(2):
                        P.dma("pool", lambda e: e.dma_start(
                            out=Wo[:, hf * 8:(hf + 1) * 8, cbk * 512:(cbk + 1) * 512],
                            in_=wout[hf * 1024:(hf + 1) * 1024, cbk * 512:(cbk + 1) * 512].rearrange("(c p) n -> p c n", p=128)),
                            writes=[t_Wo[cbk]])
                P.dma("sp", lambda e: e.dma_start(out=npo[:], in_=npost.partition_broadcast(128)), writes=[t_npo])
                for tt in range(8):
                    i = tt % 2
                    tsl = slice(tt * 128, (tt + 1) * 128)
                    P.dma("sp", lambda e: e.dma_start(out=xres[i][:], in_=xr[HALO + tt * 128:HALO + (tt + 1) * 128, :]),
                          writes=[t_xres[i]])
                    for cbk in range(4):
                        P_acc, t_acc = next_acc2()
                        for k in range(16):
                            P.op("pe", lambda e: e.matmul(P_acc[:, :], lhsT=mgT[:, k, tsl], rhs=Wo[:, k, cbk * 512:(cbk + 1) * 512],
                                                          start=(k == 0), stop=(k == 15)),
                                 reads=[t_mgT[k], t_Wo[cbk]], writes=[t_acc])
                        if cbk % 2 == 0:
                            P.op("act", lambda e: e.copy(out=ob[i][:, cbk * 512:(cbk + 1) * 512], in_=P_acc[:, :]),
                                 reads=[t_acc], writes=[t_ob[i]])
                        else:
                            P.op("dve", lambda e: e.tensor_copy(out=ob[i][:, cbk * 512:(cbk + 1) * 512], in_=P_acc[:, :]),
                                 reads=[t_acc], writes=[t_ob[i]])
                    S2 = so2[i]; tS = t_so2[i]
                    P.op("act", lambda e: e.activation(out=jk[:], in_=ob[i][:], func=AF.Square, accum_out=S2[:, 0:1]),
                         reads=[t_ob[i]], writes=[t_jk, tS])
                    P.op("act", lambda e: e.activation(out=S2[:, 1:2], in_=S2[:, 0:1], func=AF.Sqrt, scale=1.0 / D, bias=EPS),
                         reads=[tS], writes=[tS])
                    P.op("dve", lambda e: e.reciprocal(out=S2[:, 2:3], in_=S2[:, 1:2]), reads=[tS], writes=[tS])
                    P.op("dve", lambda e: e.scalar_tensor_tensor(out=ob[i][:], in0=ob[i][:], scalar=S2[:, 2:3], in1=npo[:],
                                                                 op0=ALU.mult, op1=ALU.mult),
                         reads=[tS, t_npo], writes=[t_ob[i]])
                    P.op("pool", lambda e: e.tensor_tensor(out=ob[i][:], in0=ob[i][:], in1=xres[i][:], op=ALU.add),
                         reads=[t_xres[i]], writes=[t_ob[i]])
                    P.dma("sp", lambda e: e.dma_start(out=out[tsl, :], in_=ob[i][:]), reads=[t_ob[i]], kind="r")
                P.wait_all("sp", t_ob)
                nc.gpsimd.wait_ge(ccsem, 4)
        return nc, P.signals(), P.ninstr


_CACHE = {}


def _get_nc():
    if "nc" not in _CACHE:
        _, sig, _ = build(None)
        nc, _, n = build(sig)
        _CACHE["nc"] = nc
    return _CACHE["nc"]


def _prep_inputs(x, norm_pre_w, w_in, mlstm_i_bias, mlstm_f_bias, qk_conv_w, qk_conv_b,
                 mlstm_norm_w, pool_w, pool_scale, w_proj_mlstm, w_proj_pool, w_out, norm_post_w):
    f = lambda a: np.ascontiguousarray(np.asarray(a, dtype=np.float32))
    x = f(x); w_in0 = f(w_in)[0]
    ident = np.eye(128, dtype=np.float32)
    mask = np.triu(np.ones((128, 128), np.float32))
    sel = np.zeros((2, 256), np.float32); sel[0, 0:128] = 1.0; sel[1, 128:256] = 1.0
    npw_t = f(f(norm_pre_w)[0].reshape(16, 128).T)
    w2 = f(w_in0[:, 10256:16400])
    pw = f(f(pool_w)[0].reshape(1024, 256))
    psc_t = f(f(pool_scale)[0].reshape(8, 128).T)
    wpm = f(f(w_proj_mlstm)[0]); wpp = f(f(w_proj_pool)[0]); wo = f(f(w_out)[0])
    npost = f(f(norm_post_w)[0].reshape(1, D))
    cwf = f(qk_conv_w)[0]; cbf = f(qk_conv_b)[0]
    in_maps = []
    for c in range(8):
        b, j = divmod(c, 4)
        heads = (2 * j, 2 * j + 1)
        cols = []
        for base in (0, 2048, 4096, 6144, 8192):
            for h in heads:
                cols.append(np.arange(base + h * 256, base + (h + 1) * 256))
        cols = np.concatenate(cols)
        w1 = f(w_in0[:, cols])
        wif = f(w_in0[:, [10240 + heads[0], 10240 + heads[1], 10248 + heads[0], 10248 + heads[1]]])
        qkcols = cols[:1024]
        cw_c = cwf[:, qkcols]
        cb_c = cbf[qkcols]
        cw_l = f(cw_c.reshape(4, 8, 128).transpose(2, 1, 0).reshape(128, 32))
        cb_l = f(cb_c.reshape(8, 128).T)
        gbias = f(np.stack([f(mlstm_i_bias)[0][list(heads)], f(mlstm_f_bias)[0][list(heads)]], axis=1))
        nwa = f(f(mlstm_norm_w)[0][heads[0] * 256:(heads[1] + 1) * 256].reshape(1, 512))
        xr = np.zeros((HALO + TOK2, D), np.float32)
        xr[HALO:] = x[b, j * TOK2:(j + 1) * TOK2]
        if j > 0:
            xr[:HALO] = x[b, j * TOK2 - HALO:j * TOK2]
        invc = np.zeros((4, 16), np.float32)
        for g in range(4):
            for t in range(16):
                invc[g, t] = 1.0 / (min(t + 1, WIN[g]) if j == 0 else WIN[g])
        invc = f(np.broadcast_to(invc.reshape(1, 64), (128, 64)))
        in_maps.append(dict(xb=f(x[b]), xr=xr, w1=w1, wif=wif, npw_t=npw_t, cw=cw_l, cb=cb_l, gbias=gbias,
                            nwa=nwa, ident=ident, mask=mask, sel=sel, w2=w2, pw=pw, psc_t=psc_t, invc=invc,
                            wpm=wpm, wpp=wpp, wout=wo, npost=npost))
    return in_maps


def kernel(**inputs):
    in_maps = _prep_inputs(**inputs)
    nc = _get_nc()
    if STAGE < 3:
        for m in in_maps:
            for k in ("w2", "wpm", "wpp", "wout"):
                m[k] = np.zeros((1, 1), np.float32)
    res = run_bass_kernel_spmd(nc, in_maps, core_ids=list(range(8)))
    outp = np.zeros((2, SEQ, D), np.float32)
    for c in range(8):
        b, j = divmod(c, 4)
        outp[b, j * TOK2:(j + 1) * TOK2] = res.results[c]["out"]
    if DEBUG:
        kernel.debug = [res.results[c] for c in range(8)]
    return outp
```
